# Optimizing a Trainium2 kernel written in Bass

```python
import math
import jax, jax.numpy as jnp
from jax import lax
import numpy as np

D_MODEL = 1024
BATCH = 8
SEQ = 8192
DEPTH = 2
DEC_BATCH = 4
DEC_SEQ = 4096
PAST_LEN = 128

GRID_W = 64
N_MIXERS = 2
N_POOL_LAYERS = (DEPTH + 1) // 2
N_NA_LAYERS = DEPTH // 2
POOL_EXPAND = 2
POOL_WIDTH = POOL_EXPAND * D_MODEL
POOL_WINDOWS = (2, 4, 8, 16)
N_POOL_GROUPS = len(POOL_WINDOWS)
POOL_GROUP_W = POOL_WIDTH // N_POOL_GROUPS
NA_WIDTH = D_MODEL
NA_HEAD_DIM = 32
NA_HEADS = NA_WIDTH // NA_HEAD_DIM
WIN_H = 8
WIN_W = 16
NA_SCALE = NA_HEAD_DIM ** -0.5
LN_EPS = 1e-5
DEEPNORM_ALPHA = (2 * DEPTH) ** 0.25
DEEPNORM_BETA = (8 * DEPTH) ** -0.25

kernel_name = 'hybrid_pool_natten_deepnorm_encoder'


def layer_norm(x, g, b):
    x32 = x.astype(jnp.float32)
    mu = jnp.mean(x32, axis=-1, keepdims=True)
    var = jnp.mean(jnp.square(x32 - mu), axis=-1, keepdims=True)
    y = (x32 - mu) * lax.rsqrt(var + LN_EPS) * g.astype(jnp.float32) + b.astype(jnp.float32)
    return y.astype(x.dtype)


def centred_mean_minus_self(u, w):
    seq = u.shape[1]
    h = w // 2
    u32 = u.astype(jnp.float32)
    cs = jnp.pad(jnp.cumsum(u32, axis=1), ((0, 0), (1, 0), (0, 0)))
    ext = jnp.pad(cs, ((0, 0), (h, h), (0, 0)), mode='edge')
    t = jnp.arange(seq)
    cnt = (jnp.minimum(t + h, seq) - jnp.maximum(t - h, 0)).astype(jnp.float32)
    mean = (ext[:, w:w + seq] - ext[:, :seq]) / cnt[None, :, None]
    return (mean - u32).astype(u.dtype)


def pool_branch(x, w_in, w_grp, scale, w_out):
    bsz, seq, _ = x.shape
    u, gate = jnp.split(x @ w_in, 2, axis=-1)
    ug = u.reshape(bsz, seq, N_POOL_GROUPS, POOL_GROUP_W)
    mixed = jnp.stack([centred_mean_minus_self(ug[:, :, i], POOL_WINDOWS[i])
                       for i in range(N_POOL_GROUPS)], axis=2)
    y = jnp.einsum('bsgc,gcd->bsgd', mixed, w_grp).reshape(bsz, seq, POOL_WIDTH) * scale
    return (y * jax.nn.silu(gate)) @ w_out


def neighbourhood_attention(q, k, v, rpb):
    bsz, seq = q.shape[0], q.shape[1]
    rows = seq // GRID_W
    kh = min(WIN_H, rows)
    row_start = np.clip(np.arange(rows) - kh // 2, 0, rows - kh)
    col_start = np.clip(np.arange(GRID_W) - WIN_W // 2, 0, GRID_W - WIN_W)
    col_idx = col_start[:, None] + np.arange(WIN_W)[None, :]
    dc = col_idx - np.arange(GRID_W)[:, None] + (WIN_W - 1)
    dr = row_start[:, None] + np.arange(kh)[None, :] - np.arange(rows)[:, None] + (WIN_H - 1)
    grid = lambda t: t.reshape(bsz, rows, GRID_W, NA_HEADS, NA_HEAD_DIM)
    qg, kg, vg = grid(q), grid(k), grid(v)
    rpb32 = rpb.astype(jnp.float32)

    def row_block(args):
        q_r, r0, dr_r = args
        k_w = lax.dynamic_slice_in_dim(kg, r0, kh, axis=1)[:, :, col_idx]
        v_w = lax.dynamic_slice_in_dim(vg, r0, kh, axis=1)[:, :, col_idx]
        s = jnp.einsum('bqhd,biqjhd->bhqij', q_r, k_w).astype(jnp.float32) * NA_SCALE
        bias = jnp.take(rpb32, dr_r, axis=1)[:, :, dc]
        s = s + jnp.transpose(bias, (0, 2, 1, 3))[None]
        p = jax.nn.softmax(s.reshape(bsz, NA_HEADS, GRID_W, kh * WIN_W), axis=-1).reshape(s.shape)
        return jnp.einsum('bhqij,biqjhd->bqhd', p.astype(v_w.dtype), v_w)

    out = lax.map(row_block, (jnp.moveaxis(qg, 1, 0),
                              jnp.asarray(row_start, dtype=jnp.int32),
                              jnp.asarray(dr, dtype=jnp.int32)))
    return jnp.moveaxis(out, 0, 1).reshape(bsz, seq, NA_WIDTH)


def na_branch(x, w_in, rpb, w_out):
    bsz, seq, _ = x.shape
    q, k, v, gate = jnp.split(x @ w_in, 4, axis=-1)
    heads = lambda t: t.reshape(bsz, seq, NA_HEADS, NA_HEAD_DIM)
    o = neighbourhood_attention(heads(q), heads(k), heads(v), rpb)
    return (o * jax.nn.silu(gate)) @ w_out


def trunk(x, w_in_pool, w_grp_pool, scale_pool, w_out_pool, w_in_na, rpb_na, w_out_na, ln_g, ln_b):
    for i in range(DEPTH):
        j = i // N_MIXERS
        if i % N_MIXERS == 0:
            h = pool_branch(x, w_in_pool[j], w_grp_pool[j], scale_pool[j], w_out_pool[j])
        else:
            h = na_branch(x, w_in_na[j], rpb_na[j], w_out_na[j])
        x = layer_norm(DEEPNORM_ALPHA * x + h, ln_g[i], ln_b[i])
    return x


def setup_inputs(seed: int = 0) -> dict:
    key = jax.random.key(seed)
    ks = jax.random.split(key, 11)
    f32 = jnp.float32
    nrm = lambda k, shape: jax.random.normal(k, shape, dtype=f32)
    return {
        'x_prompt': nrm(ks[0], (BATCH, SEQ, D_MODEL)),
        'x_sample': nrm(ks[1], (DEC_BATCH, DEC_SEQ, D_MODEL)),
        'w_in_pool': nrm(ks[2], (N_POOL_LAYERS, D_MODEL, 2 * POOL_WIDTH)) * D_MODEL ** -0.5,
        'w_grp_pool': nrm(ks[3], (N_POOL_LAYERS, N_POOL_GROUPS, POOL_GROUP_W, POOL_GROUP_W)) * POOL_GROUP_W ** -0.5,
        'scale_pool': 1.0 + 0.02 * nrm(ks[4], (N_POOL_LAYERS, POOL_WIDTH)),
        'w_out_pool': nrm(ks[5], (N_POOL_LAYERS, POOL_WIDTH, D_MODEL)) * (POOL_WIDTH ** -0.5 * DEEPNORM_BETA),
        'w_in_na': nrm(ks[6], (N_NA_LAYERS, D_MODEL, 4 * NA_WIDTH)) * D_MODEL ** -0.5,
        'rpb_na': 0.1 * nrm(ks[7], (N_NA_LAYERS, NA_HEADS, 2 * WIN_H - 1, 2 * WIN_W - 1)),
        'w_out_na': nrm(ks[8], (N_NA_LAYERS, NA_WIDTH, D_MODEL)) * (NA_WIDTH ** -0.5 * DEEPNORM_BETA),
        'ln_g': 1.0 + 0.02 * nrm(ks[9], (DEPTH, D_MODEL)),
        'ln_b': 0.02 * nrm(ks[10], (DEPTH, D_MODEL)),
    }


def reference(x_prompt, x_sample, w_in_pool, w_grp_pool, scale_pool, w_out_pool,
              w_in_na, rpb_na, w_out_na, ln_g, ln_b):
    y_prompt = trunk(x_prompt, w_in_pool, w_grp_pool, scale_pool, w_out_pool,
                     w_in_na, rpb_na, w_out_na, ln_g, ln_b)
    y_sample = trunk(x_sample, w_in_pool, w_grp_pool, scale_pool, w_out_pool,
                     w_in_na, rpb_na, w_out_na, ln_g, ln_b)
    return (y_prompt, y_sample)
```

```python
import numpy as np
import concourse.bass as bass
import concourse.mybir as mybir
from concourse.bass_utils import run_bass_kernel_spmd

F32 = mybir.dt.float32
BF16 = mybir.dt.bfloat16
AF = mybir.ActivationFunctionType
ALU = mybir.AluOpType

D = 1024
GRID_W = 64
POOL_WINDOWS = (2, 4, 8, 16)
NA_SCALE = 32 ** -0.5
LN_EPS = 1e-5
ALPHA = 4 ** 0.25
NEG = -30000.0
UOFF = 16
UW = 544
N_CORES = 8
DEBUG = False
OPT_POOL = False
OPT_ROT = True
OPT_SQRT = True
OPT_RECIP = False
OPT_PEPOOL = True
ST_ENG = "pool"


class Prog:
    def __init__(self, nc, sems):
        self.nc = nc
        self.sems = sems
        self.active = None
        self.cur = None

    def begin(self, active, eng):
        self.active = active
        self.cur = eng
        self.cnt = {k: 0 for k in self.sems}
        self.waited = {}

    def op(self, eng, meth, *a, sig=True, **kw):
        ins = None
        if eng == self.active:
            ins = getattr(self.cur, meth)(*a, **kw)
        if sig:
            self.cnt[eng] += 1
            if ins is not None:
                ins.then_inc(self.sems[eng], 1)
            return (eng, self.cnt[eng])
        return None

    def wait(self, eng, *evs):
        for ev in evs:
            if ev is None:
                continue
            key, n = ev
            if self.waited.get((eng, key), 0) >= n:
                continue
            self.waited[(eng, key)] = n
            if eng == self.active:
                self.cur.wait_ge(self.sems[key], n)

    def dma(self, eng, semkey, out, in_):
        if eng == self.active:
            self.cur.dma_start(out=out, in_=in_).then_inc(self.sems[semkey], 16)
        self.cnt[semkey] += 16
        return (semkey, self.cnt[semkey])


class Arena:
    def __init__(self, nc, words):
        self.t = nc.alloc_sbuf_tensor("arena", [128, words], F32)
        self.off = 0
        self.words = words

    def f32(self, n):
        assert self.off + n <= self.words, (self.off, n, self.words)
        v = self.t[:, self.off:self.off + n]
        self.off += n
        return v

    def bf16(self, n):
        assert n % 2 == 0
        return self.f32(n // 2).bitcast(BF16)


def l0_tiles(ntok):
    tiles = []
    s = 0
    while s + 512 - 16 < ntok:
        tiles.append((s, 512))
        s += 496
    rem = ntok - s
    nt = ((rem + 16 + 127) // 128) * 128
    tiles.append((ntok + 16 - nt, nt))
    return tiles


def build_program(seqs):
    nc = bass.Bass("TRN2", target_bir_lowering=False)
    dram = {}

    def din(name, shape, dt=F32):
        dram[name] = nc.dram_tensor(name, list(shape), dt, kind="ExternalInput").ap()
        return dram[name]

    def dout(name, shape):
        dram[name] = nc.dram_tensor(name, list(shape), F32, kind="ExternalOutput").ap()
        return dram[name]

    xin_d = {}
    x1_d = {}
    y_d = {}
    for name, ntok in seqs:
        xin_d[name] = din("x_" + name, [ntok + 16, D])
        x1_d[name] = (nc.dram_tensor("x1_" + name, [ntok, D], F32, kind="ExternalOutput").ap() if DEBUG
                      else nc.dram_tensor("x1_" + name, [ntok, D], F32).ap())
        y_d[name] = dout("y_" + name, [ntok, D])
    w_in_pool = din("w_in_pool", [D, 4096])
    w_grp = din("w_grp", [4, 512, 512])
    w_out_pool = din("w_out_pool", [2048, D])
    w_in_na = din("w_in_na", [D, 4096])
    w_out_na = din("w_out_na", [D, D])
    scale_t = din("scale_t", [128, 16])
    lnp = din("lnp", [128, 4, D])
    ident_d = din("ident", [128, 128])
    rtab_d = din("rtab", [128, len(seqs), 2, 4, 8])
    bm_d = din("bias_master", [128, 32, 16, 64])
    pmat_d = din("pmat", [128, len(seqs) * 5, 128])
    e_un = nc.dram_tensor("e_un", [128, 32, 16, 64], BF16).ap()
    e_int = nc.dram_tensor("e_int", [128, 32, 16, 64], BF16).ap()

    sem_names = ["pe", "act", "dve", "pool", "xin", "rb0", "rb1", "st0", "st1", "wst0", "wst1", "wst2", "wst3", "wst4", "wst5",
                 "cst", "eb0", "eb1", "misc", "x1b0", "x1b1"]
    sems = {k: nc.alloc_semaphore(k) for k in sem_names}
    P = Prog(nc, sems)
    ar = Arena(nc, 53200)
    ps = nc.psum_tensor("ps", [128, 8, 512], F32).__enter__()

    Win = ar.bf16(8 * 4096).rearrange("p (k f) -> p k f", f=4096)
    Wout = ar.bf16(16 * 1024).rearrange("p (k f) -> p k f", f=1024)
    ident = ar.f32(128)
    gb = ar.f32(2 * D).rearrange("p (a f) -> p a f", f=D)
    scale_sb = ar.f32(16)
    rtab = ar.f32(64 * len(seqs)).rearrange("p (s e w c) -> p s e w c", s=len(seqs), e=2, w=4)
    ones_bf = ar.bf16(32)
    rb = [ar.f32(D), ar.f32(D)]
    stats = [ar.f32(12).rearrange("p (a b) -> p a b", b=6) for _ in range(2)]
    mv = [ar.f32(2) for _ in range(2)]
    rstd = [ar.f32(1) for _ in range(2)]
    nmr = [ar.f32(1) for _ in range(2)]
    nwt = [ar.f32(4) for _ in range(2)]
    eps_col = ar.f32(1)
    base = ar.off
    xin = ar.f32(4 * D).rearrange("p (b f) -> p b f", f=D)
    xT = ar.bf16(8 * 512).rearrange("p (k t) -> p k t", t=512)
    Ub = [ar.f32(UW) for _ in range(3)]
    Al = [ar.f32(UW) for _ in range(3)]
    Bl = Al
    u3 = ar.bf16(4 * 512).rearrange("p (b f) -> p b f", f=512)
    Pm = ar.bf16(len(seqs) * 5 * 128).rearrange("p (k t) -> p k t", t=128)
    mixed = ar.bf16(16 * 512).rearrange("p (k t) -> p k t", t=512)
    sgr = ar.bf16(3 * 512).rearrange("p (k t) -> p k t", t=512)
    zT = ar.bf16(16 * 512).rearrange("p (k t) -> p k t", t=512)
    etmp = ar.f32(8)
    etmp2 = ar.f32(8)
    wg_off = ar.off
    Wg = ar.bf16(16 * 512).rearrange("p (k f) -> p k f", f=512)
    l0_end = ar.off
    ar.off = base
    KT = ar.bf16(8 * 1024).rearrange("p (k t) -> p k t", t=1024)
    Vr = ar.bf16(8 * 1024).rearrange("p (s f) -> p s f", f=1024)
    QT = [ar.bf16(8 * 256).rearrange("p (k t) -> p k t", t=256) for _ in range(2)]
    sg1 = [ar.bf16(8 * 256).rearrange("p (k t) -> p k t", t=256) for _ in range(2)]
    aT1 = ar.bf16(8 * 256).rearrange("p (k t) -> p k t", t=256)
    aT = [aT1, aT1]
    x1b = [ar.f32(D), ar.f32(D)]
    x1T = [ar.bf16(8 * 256).rearrange("p (k t) -> p k t", t=256) for _ in range(2)]
    Eb = [ar.bf16(4 * 640).rearrange("p (h c) -> p h c", c=640) for _ in range(2)]
    expS = [ar.bf16(4 * 256).rearrange("p (h c) -> p h c", c=256) for _ in range(2)]
    PT = [ar.bf16(4 * 256).rearrange("p (h c) -> p h c", c=256) for _ in range(4)]
    rden = ar.f32(256)
    tmul = ar.f32(256)
    thb1 = ar.f32(256)
    thb = [thb1, thb1]
    l1_end = ar.off
    assert l1_end <= ar.words, (l1_end, ar.words)
    ar.off = base
    NSTG = 6
    stg = [ar.f32(2048) for _ in range(NSTG)]
    stg_end = ar.off
    ar.off = base
    bstage = ar.f32(8 * 1024)
    estage = ar.bf16(8 * 1024).rearrange("p (h j c) -> p h j c", h=8, j=16)
    assert max(ar.off, stg_end) <= wg_off, (ar.off, stg_end, wg_off)

    dumps = {}
    if DEBUG == "l0t0":
        for nm, ap_, dt_ in (("xT", xT, BF16), ("Ub0", Ub[0], F32), ("mixed", mixed, BF16), ("sgr", sgr, BF16),
                             ("zT", zT, BF16), ("rb0", rb[0], F32), ("rb1", rb[1], F32), ("Win0", Win[:, 0, :], BF16),
                             ("Wg", Wg, BF16), ("Wout", Wout, BF16), ("scale_sb", scale_sb, F32), ("xin", xin, F32),
                             ("mv0", mv[0], F32), ("rstd0", rstd[0], F32), ("gb", gb, F32), ("Al3", Al[2], F32),
                             ("u3", u3, BF16), ("Pm", Pm, BF16)):
            dumps[nm] = (nc.dram_tensor("dbg_" + nm, list(ap_.shape), dt_, kind="ExternalOutput").ap(), ap_)

    def do_dumps():
        last = {k: (k, P.cnt[k]) for k in ("pe", "act", "dve", "pool")}
        for k in last:
            if last[k][1] > 0:
                P.wait("sp", last[k])
        P.wait("sp", st_free_g[0], st_free_g[1])
        ev = None
        for nm, (d_, a_) in dumps.items():
            ev = P.dma("sp", "misc", d_, a_)
        P.wait("sp", ev)

    st_free_g = [None, None]

    def program():
        c_ev = []
        c_ev.append(P.dma("sp", "cst", ident, ident_d))
        c_ev.append(P.dma("sp", "cst", gb, lnp[:, 0:2, :]))
        c_ev.append(P.dma("sp", "cst", scale_sb, scale_t))
        c_ev.append(P.dma("sp", "cst", rtab, rtab_d))
        cst_ev = c_ev[-1]
        for e in ("pe", "act", "dve", "pool"):
            P.wait(e, cst_ev)
        ones_ev = P.op("dve", "memset", ones_bf, 1.0)
        ones_ev = P.op("dve", "memset", eps_col, LN_EPS)
        P.wait("act", ones_ev)

        wst_free = [None] * NSTG
        wcount = [0]
        cast_engs = ["act", "dve", "pool"]

        def load_cast(dst_ap, src_ap, n):
            i = wcount[0]
            wcount[0] += 1
            s = i % NSTG
            P.wait("sp", wst_free[s])
            ev = P.dma("sp", "wst%d" % s, stg[s][:, 0:n] if len(src_ap.shape) == 2 else
                       stg[s][:, 0:n].rearrange("p (a b) -> p a b", b=src_ap.shape[2]), src_ap)
            ce = cast_engs[i % 3]
            P.wait(ce, ev)
            src = stg[s][:, 0:n]
            if len(dst_ap.shape) == 3:
                src = src.rearrange("p (a b) -> p a b", b=dst_ap.shape[2])
            if ce == "act":
                cev = P.op("act", "activation", dst_ap, src, AF.Copy)
            else:
                cev = P.op(ce, "tensor_copy", dst_ap, src)
            wst_free[s] = cev
            return cev

        def load_weights(layer):
            evs = []
            w_in = w_in_pool if layer == 0 else w_in_na
            wv = w_in.rearrange("(k p) f -> p k f", p=128)
            for k in range(8):
                for hf in range(2):
                    evs.append(load_cast(Win[:, k, hf * 2048:(hf + 1) * 2048],
                                         wv[:, k, hf * 2048:(hf + 1) * 2048], 2048))
            if layer == 0:
                gv = w_grp.rearrange("g (kc p) d -> p (g kc) d", p=128)
                for q in range(4):
                    evs.append(load_cast(Wg[:, q * 4:(q + 1) * 4, :], gv[:, q * 4:(q + 1) * 4, :], 2048))
                ov = w_out_pool.rearrange("(k p) d -> p k d", p=128)
                for q in range(8):
                    evs.append(load_cast(Wout[:, q * 2:(q + 1) * 2, :], ov[:, q * 2:(q + 1) * 2, :], 2048))
            else:
                ov = w_out_na.rearrange("(k p) d -> p k d", p=128)
                for q in range(4):
                    evs.append(load_cast(Wout[:, q * 2:(q + 1) * 2, :], ov[:, q * 2:(q + 1) * 2, :], 2048))
            return evs

        bview = bstage.rearrange("p (h j c) -> p h j c", h=8, j=16)
        d2 = None
        for q in range(4):
            P.wait("sp", d2)
            ev = P.dma("sp", "misc", bview, bm_d[:, q * 8:(q + 1) * 8, :, :])
            P.wait("act", ev, d2)
            xe = P.op("act", "activation", estage, bview, AF.Exp)
            P.wait("pool", xe)
            d1 = P.dma("pool", "misc", e_un[:, q * 8:(q + 1) * 8, :, :], estage)
            P.wait("dve", d1, xe)
            m1 = P.op("dve", "memset", estage[0:64, :, 0:4, :], 0.0)
            m1 = P.op("dve", "memset", estage[0:64, :, 12:16, :], 0.0)
            m1 = P.op("dve", "memset", estage[64:128, :, 0:5, :], 0.0)
            m1 = P.op("dve", "memset", estage[64:128, :, 13:16, :], 0.0)
            P.wait("pool", m1)
            d2 = P.dma("pool", "misc", e_int[:, q * 8:(q + 1) * 8, :, :], estage)
        for e in ("sp", "act", "dve", "pool"):
            P.wait(e, d2)

        w0 = load_weights(0)
        for e in ("pe", "act", "dve", "pool", "sp"):
            P.wait(e, *w0[-3:])
        P.wait("sp", wst_free[0])
        pm_ld = P.dma("sp", "wst0", stg[0][:, 0:len(seqs) * 640].rearrange("p (k t) -> p k t", t=128), pmat_d)
        P.wait("dve", pm_ld)
        pm_ev = P.op("dve", "tensor_copy", Pm, stg[0][:, 0:len(seqs) * 640].rearrange("p (k t) -> p k t", t=128))
        wst_free[0] = pm_ev
        P.wait("pe", pm_ev)
        P.wait("sp", pm_ev)
        for u in Ub + Al:
            zl_ev = P.op("dve", "memset", u, 0.0)
        P.wait("act", zl_ev)

        st_free = [None, None]
        blk_counter = [0]

        def ln_tail(slot, layer, ps_evs, ps_banks, x_ev, dst_ap, p0, p1, last_reader_cb=None):
            r = rb[slot]
            e = None
            for dh in range(2):
                P.wait("dve", ps_evs[dh], x_ev)
                e = P.op("dve", "scalar_tensor_tensor", out=r[:, dh * 512:(dh + 1) * 512],
                         in0=r[:, dh * 512:(dh + 1) * 512], scalar=ALPHA,
                         in1=ps[:, ps_banks[dh], :], op0=ALU.mult, op1=ALU.add)
                if last_reader_cb:
                    last_reader_cb(dh, e)
                P.wait("dve", e)
                e = P.op("dve", "bn_stats", stats[slot][:, dh, :], r[:, dh * 512:(dh + 1) * 512])
            P.wait("dve", e)
            e = P.op("dve", "bn_aggr", mv[slot], stats[slot].rearrange("p a b -> p (a b)"))
            use_sqrt = OPT_SQRT
            if use_sqrt:
                P.wait("act", e)
                e = P.op("act", "activation", nwt[slot][:, 0:1], mv[slot][:, 1:2], AF.Sqrt, bias=eps_col, scale=1.0)
                P.wait("dve", e)
                e = P.op("dve", "reciprocal", rstd[slot], nwt[slot][:, 0:1])
            P.wait("dve", e)
            if not use_sqrt:
                e = P.op("dve", "tensor_scalar", nwt[slot][:, 0:1], mv[slot][:, 1:2], LN_EPS, None, ALU.add)
                P.wait("dve", e)
                e = P.op("dve", "tensor_scalar", nwt[slot][:, 1:2], nwt[slot][:, 0:1], 0.5, 0.5, ALU.mult, ALU.add)
                P.wait("dve", e)
                e = P.op("dve", "reciprocal", rstd[slot], nwt[slot][:, 1:2])
            for _ in range(0 if use_sqrt else 5):
                P.wait("dve", e)
                e = P.op("dve", "scalar_tensor_tensor", out=nwt[slot][:, 1:2], in0=rstd[slot],
                         scalar=nwt[slot][:, 0:1], in1=rstd[slot], op0=ALU.mult, op1=ALU.mult)
                P.wait("dve", e)
                e = P.op("dve", "tensor_scalar", nwt[slot][:, 2:3], nwt[slot][:, 1:2], -0.5, 1.5, ALU.mult, ALU.add)
                P.wait("dve", e)
                e = P.op("dve", "tensor_tensor", rstd[slot], rstd[slot], nwt[slot][:, 2:3], ALU.mult)
            P.wait("dve", e)
            e = P.op("dve", "scalar_tensor_tensor", out=nmr[slot], in0=mv[slot][:, 0:1], scalar=-1.0,
                     in1=rstd[slot], op0=ALU.mult, op1=ALU.mult)
            P.wait("act", e)
            e = P.op("act", "activation", r, r, AF.Identity, bias=nmr[slot], scale=rstd[slot])
            P.wait("pool", e)
            e = P.op("pool", "tensor_tensor", r, r, gb[:, 0, :], ALU.mult)
            P.wait("pool", e)
            e = P.op("pool", "tensor_tensor", r, r, gb[:, 1, :], ALU.add)
            P.wait(ST_ENG, e)
            st_free[slot] = P.dma(ST_ENG, "st%d" % slot, dst_ap, r[p0:p1, :])
            st_free_g[slot] = st_free[slot]

        bank_free = [None] * 8
        xin_free = None
        mixed_rd = [None] * 16
        zT_rd = None
        sg_rd = [None] * 3
        ub_rd = [None] * 3
        u3_rd = [None]
        al_rd = {"dve": None, "pool": None}
        ucnt = {"dve": 0, "pool": 0}
        xT_ready = None

        tiles = []
        seq_idx = {}
        for sidx, (name, ntok) in enumerate(seqs):
            seq_idx[name] = sidx
            tl = l0_tiles(ntok)
            for ti, (s, nt) in enumerate(tl):
                tiles.append((name, ntok, s, nt, ti == 0, ti == len(tl) - 1))

        def issue_xin_load(t):
            name, ntok, s, nt, _, _ = tiles[t]
            nb = nt // 128
            P.wait("sp", xin_free)
            return P.dma("sp", "xin", xin[:, 0:nb, :],
                         xin_d[name][s:s + nt, :].rearrange("(b p) f -> p b f", p=128))

        def transposes(t, ld_ev):
            nonlocal xin_free
            name, ntok, s, nt, _, _ = tiles[t]
            nb = nt // 128
            evs = []
            P.wait("pe", ld_ev)
            for dk in range(8):
                bank = 6 + dk % 2
                P.wait("pe", bank_free[bank])
                pe_ev = None
                for b in range(nb):
                    pe_ev = P.op("pe", "transpose", ps[:, bank, b * 128:(b + 1) * 128],
                                 xin[:, b, dk * 128:(dk + 1) * 128], ident, sig=(b == nb - 1))
                P.wait("act", pe_ev)
                ev = P.op("act", "activation", xT[:, dk, 0:nt], ps[:, bank, 0:nt], AF.Copy)
                bank_free[bank] = ev
                evs.append(ev)
            xin_free = pe_ev
            return evs[-1]

        rb_load_ev = {}

        def issue_rb_load(q, src_ap, p0, p1):
            slot = q % 2
            P.wait("sp", st_free[slot])
            rb_load_ev[q] = P.dma("sp", "rb%d" % slot, rb[slot][p0:p1, :], src_ap)

        def l0_rows(t, tb):
            name, ntok, s, nt, _, _ = tiles[t]
            lo = max(tb * 128, 8)
            hi = min(tb * 128 + 128, nt - 8)
            return name, s, lo - tb * 128, hi - tb * 128, s + lo - 8, s + hi - 8

        blocks0 = []
        for t, (name, ntok, s, nt, _, _) in enumerate(tiles):
            for tb in range(nt // 128):
                blocks0.append((t, tb))

        def issue_rb_load_l0(q):
            if q >= len(blocks0):
                return
            t, tb = blocks0[q]
            name, s, p0, p1, t0, t1 = l0_rows(t, tb)
            issue_rb_load(q, xin_d[name][s + tb * 128:s + tb * 128 + 128, :], 0, 128)

        ld = issue_xin_load(0)
        xT_ready = transposes(0, ld)
        issue_rb_load_l0(0)
        issue_rb_load_l0(1)
        q0 = 0

        for t, (name, ntok, s, nt, is_first, is_last) in enumerate(tiles):
            nb = nt // 128
            if t + 1 < len(tiles):
                ld_next = issue_xin_load(t + 1)
            P.wait("pe", xT_ready)
            m1_last = None
            mixed_ev = [None] * 16
            order = [0, 8, 4, 12, 1, 9, 5, 13, 2, 10, 6, 14, 3, 11, 7, 15]
            pe_pool_items = []
            if OPT_PEPOOL:
                order = [0, 8, 4, 1, 9, 5, 2, 10, 6, 3, 11, 7]
                sidx = seq_idx[name]
                for tb in range(nb):
                    bank = tb % 2
                    P.wait("pe", bank_free[bank])
                    for dk in range(8):
                        pe_ev = P.op("pe", "matmul", ps[:, bank, :], lhsT=xT[:, dk, tb * 128:(tb + 1) * 128],
                                     rhs=Win[:, dk, 1536:2048], start=(dk == 0), stop=(dk == 7), sig=(dk == 7))
                    P.wait("act", pe_ev, u3_rd[0] if tb == 0 else None)
                    u3_ev = P.op("act", "activation", u3[:, tb, :], ps[:, bank, :], AF.Copy)
                    bank_free[bank] = u3_ev

                def pe_pool(c, u3_ev=u3_ev):
                    fc = 12 + c
                    bank = 2 + c % 2
                    P.wait("pe", bank_free[bank], u3_ev)
                    pe_ev = None
                    for b in range(nb):
                        srcs = [bb for bb in (b - 1, b, b + 1) if 0 <= bb < nb]
                        for k, bb in enumerate(srcs):
                            if bb == b:
                                mi = 3 if (is_first and b == 0) else (4 if (is_last and b == nb - 1) else 1)
                            else:
                                mi = 0 if bb < b else 2
                            pe_ev = P.op("pe", "matmul", ps[:, bank, b * 128:(b + 1) * 128],
                                         lhsT=u3[:, bb, c * 128:(c + 1) * 128], rhs=Pm[:, sidx * 5 + mi, :],
                                         start=(k == 0), stop=(k == len(srcs) - 1),
                                         sig=(b == nb - 1 and k == len(srcs) - 1))
                    u3_rd[0] = pe_ev
                    P.wait("act", pe_ev, mixed_rd[fc])
                    ev = P.op("act", "activation", mixed[:, fc, 0:nt], ps[:, bank, 0:nt], AF.Copy)
                    bank_free[bank] = ev
                    mixed_ev[fc] = ev
                pe_pool_items = [lambda c=c: pe_pool(c) for c in range(4)]
            for oi, fc in enumerate(order):
                if pe_pool_items and oi in (2, 4, 6, 8):
                    pe_pool_items.pop(0)()
                bank = oi % 2
                P.wait("pe", bank_free[bank])
                for dk in range(8):
                    pe_ev = P.op("pe", "matmul", ps[:, bank, 0:nt], lhsT=Win[:, dk, fc * 128:(fc + 1) * 128],
                                 rhs=xT[:, dk, 0:nt], start=(dk == 0), stop=(dk == 7), sig=(dk == 7))
                g = fc // 4
                w = POOL_WINDOWS[g]
                h = w // 2
                on_pool = OPT_POOL and g < 2
                eng = "pool" if on_pool else "dve"
                ui_ = ucnt[eng] % 3
                ucnt[eng] += 1
                u = Ub[ui_]
                lv = Bl if on_pool else Al
                et = etmp2 if on_pool else etmp
                P.wait("act", pe_ev, ub_rd[ui_])
                cp = P.op("act", "activation", u[:, UOFF:UOFF + nt], ps[:, bank, 0:nt], AF.Copy)
                bank_free[bank] = cp
                P.wait(eng, cp, al_rd[eng], mixed_rd[fc])
                cur = u
                e = None
                lvl = 0
                sh = 1
                while sh < h:
                    lo = UOFF - h
                    hi = UOFF + nt + max(h - 2 * sh, 0)
                    e = P.op(eng, "tensor_tensor", lv[lvl][:, lo:hi], cur[:, lo:hi], cur[:, lo + sh:hi + sh], ALU.add)
                    P.wait(eng, e)
                    cur = lv[lvl]
                    lvl += 1
                    sh *= 2
                S = lv[lvl] if lvl < len(lv) else lv[0]
                e = P.op(eng, "tensor_tensor", S[:, UOFF:UOFF + nt], cur[:, UOFF - h:UOFF - h + nt],
                         cur[:, UOFF:UOFF + nt], ALU.add)
                P.wait(eng, e)
                if on_pool:
                    e = P.op(eng, "tensor_scalar", S[:, UOFF:UOFF + nt], S[:, UOFF:UOFF + nt], 1.0 / w, None, ALU.mult)
                    P.wait(eng, e)
                    e = P.op(eng, "tensor_tensor", mixed[:, fc, 0:nt], S[:, UOFF:UOFF + nt], u[:, UOFF:UOFF + nt],
                             ALU.subtract)
                    sfac = float(w)
                else:
                    e = P.op(eng, "scalar_tensor_tensor", out=mixed[:, fc, 0:nt], in0=S[:, UOFF:UOFF + nt],
                             scalar=1.0 / w, in1=u[:, UOFF:UOFF + nt], op0=ALU.mult, op1=ALU.subtract)
                    sfac = 1.0
                for edge, on, c0 in ((0, is_first, 8), (1, is_last, nt - 16)):
                    if on:
                        P.wait(eng, e)
                        e = P.op(eng, "tensor_tensor", et, S[:, UOFF + c0:UOFF + c0 + 8], rtab[:, seq_idx[name], edge, g, :], ALU.mult)
                        P.wait(eng, e)
                        if sfac != 1.0:
                            e = P.op(eng, "tensor_scalar", et, et, sfac, None, ALU.mult)
                            P.wait(eng, e)
                        e = P.op(eng, "tensor_tensor", mixed[:, fc, c0:c0 + 8], et,
                                 u[:, UOFF + c0:UOFF + c0 + 8], ALU.subtract)
                ub_rd[ui_] = e
                al_rd[eng] = e
                mixed_ev[fc] = e
            z_ev = None
            for dc in range(16):
                fc = 16 + dc
                bank = fc % 2
                P.wait("pe", bank_free[bank])
                for dk in range(8):
                    pe_ev = P.op("pe", "matmul", ps[:, bank, 0:nt], lhsT=Win[:, dk, fc * 128:(fc + 1) * 128],
                                 rhs=xT[:, dk, 0:nt], start=(dk == 0), stop=(dk == 7), sig=(dk == 7))
                m1_last = pe_ev
                P.wait("act", pe_ev, sg_rd[dc % 3])
                sg_ev = P.op("act", "activation", sgr[:, dc % 3, 0:nt], ps[:, bank, 0:nt], AF.Silu)
                bank_free[bank] = sg_ev
                g = dc // 4
                bank2 = 2 + dc % 2
                P.wait("pe", bank_free[bank2], *mixed_ev[4 * g:4 * g + 4])
                for kc in range(4):
                    pe2 = P.op("pe", "matmul", ps[:, bank2, 0:nt],
                               lhsT=Wg[:, g * 4 + kc, (dc % 4) * 128:(dc % 4 + 1) * 128],
                               rhs=mixed[:, g * 4 + kc, 0:nt], start=(kc == 0), stop=(kc == 3), sig=(kc == 3))
                if dc % 4 == 3:
                    for kc in range(4):
                        mixed_rd[g * 4 + kc] = pe2
                P.wait("dve", pe2, sg_ev, zT_rd)
                z_ev = P.op("dve", "scalar_tensor_tensor", out=zT[:, dc, 0:nt], in0=ps[:, bank2, 0:nt],
                            scalar=scale_sb[:, dc:dc + 1], in1=sgr[:, dc % 3, 0:nt], op0=ALU.mult, op1=ALU.mult)
                bank_free[bank2] = z_ev
                sg_rd[dc % 3] = z_ev
            if t + 1 < len(tiles):
                P.wait("act", m1_last)
                xT_ready = transposes(t + 1, ld_next)
            P.wait("pe", z_ev)
            for tb in range(nb):
                q = q0 + tb
                slot = q % 2
                pevs = []
                bpair = [(4, 5), (0, 1), (2, 3)][tb % 3] if OPT_ROT else (4, 5)
                for dh in range(2):
                    bank = bpair[dh]
                    P.wait("pe", bank_free[bank])
                    for wc in range(16):
                        pe_ev = P.op("pe", "matmul", ps[:, bank, :], lhsT=zT[:, wc, tb * 128:(tb + 1) * 128],
                                     rhs=Wout[:, wc, dh * 512:(dh + 1) * 512], start=(wc == 0), stop=(wc == 15),
                                     sig=(wc == 15))
                    pevs.append(pe_ev)
                zT_rd = pe_ev
                _, _, p0, p1, t0, t1 = l0_rows(t, tb)

                def cb(dh, e, bpair=bpair):
                    bank_free[bpair[dh]] = e
                ln_tail(slot, 0, pevs, list(bpair), rb_load_ev[q], x1_d[name][t0:t1, :], p0, p1, cb)
                issue_rb_load_l0(q + 2)
            q0 += nb
            if DEBUG == "l0t0":
                do_dumps()
                return

        P.wait("sp", st_free[0], st_free[1])
        P.wait("sp", zT_rd)
        for e in ("act", "dve", "pool"):
            P.wait(e, zT_rd, st_free[0], st_free[1])
        gl = P.dma("sp", "cst", gb, lnp[:, 2:4, :])
        w1 = load_weights(1)
        for e in ("act", "dve", "pool"):
            P.wait(e, gl)
        for e in ("pe", "act", "dve", "pool", "sp"):
            P.wait(e, *w1[-3:])
        z1 = P.op("dve", "memset", KT, 0.0)
        z1 = P.op("dve", "memset", Vr, 0.0)
        for e in ("pe", "act", "pool"):
            P.wait(e, z1)

        gen_bank = [6, 7]
        gb_i = [0]

        def next_bank():
            b = gen_bank[gb_i[0] % 2]
            gb_i[0] += 1
            return b

        th_rd = [None, None]
        kt_rd = [None] * 4
        v_rd = [None] * 8
        x1b_rd = [None, None]
        x1T_rd = [None, None]
        x1T_ev = [None, None]
        qt_rd = [None, None]
        sg1_rd = [None, None]
        aT_rd = [None, None]
        eb_rd = [None, None]
        es_rd = [None, None]
        pt_rd = [None] * 4
        od_rd = [None, None]
        st_free_b = [None, None]
        kv_ev = {}
        q_ev = {}
        g_ev = {}
        a_evs = {}
        blkq = [0]
        x1b_i = [0]
        stepc = [0]
        hgc = [0]
        LAG = 3

        for name, ntok in seqs:
            nrows = ntok // GRID_W
            nblk = nrows // 4
            x1s = x1_d[name]
            kv_ev.clear(); q_ev.clear(); g_ev.clear(); a_evs.clear()

            def items_proj_kv(B):
                s = B % 2
                slot = B % 4
                items = []

                def tr(sb, half):
                    def f():
                        if half == 0:
                            i = x1b_i[0]
                            x1b_i[0] += 1
                            bs = i % 2
                            tr.bs = bs
                            P.wait("sp", x1b_rd[bs])
                            tr.ld = P.dma("sp", "x1b%d" % bs, x1b[bs], x1s[B * 256 + sb * 128:B * 256 + sb * 128 + 128, :])
                            if sb == 0:
                                P.wait("act", x1T_rd[s])
                        bs = tr.bs
                        P.wait("pe", tr.ld)
                        bank = next_bank()
                        P.wait("pe", bank_free[bank])
                        for d4 in range(4):
                            dk = half * 4 + d4
                            pe_ev = P.op("pe", "transpose", ps[:, bank, d4 * 128:(d4 + 1) * 128],
                                         x1b[bs][:, dk * 128:(dk + 1) * 128], ident, sig=(d4 == 3))
                        P.wait("act", pe_ev)
                        ev = P.op("act", "activation",
                                  x1T[s][:, half * 4:half * 4 + 4, sb * 128:(sb + 1) * 128],
                                  ps[:, bank, :].rearrange("p (a b) -> p a b", b=128), AF.Copy)
                        bank_free[bank] = ev
                        if half == 1:
                            x1b_rd[bs] = pe_ev
                        x1T_ev[s] = ev
                    return f
                for sb in range(2):
                    for half in range(2):
                        items.append(tr(sb, half))

                def kgrp(c):
                    def f():
                        P.wait("pe", x1T_ev[s])
                        bank = next_bank()
                        P.wait("pe", bank_free[bank])
                        for dk in range(8):
                            pe_ev = P.op("pe", "matmul", ps[:, bank, 0:256],
                                         lhsT=Win[:, dk, 1024 + c * 128:1024 + (c + 1) * 128],
                                         rhs=x1T[s][:, dk, :], start=(dk == 0), stop=(dk == 7), sig=(dk == 7))
                        P.wait("act", pe_ev, kt_rd[slot])
                        ev = P.op("act", "activation", KT[:, c, slot * 256:(slot + 1) * 256], ps[:, bank, 0:256], AF.Copy)
                        bank_free[bank] = ev
                        kv_ev[B] = ev
                    return f
                for c in range(8):
                    items.append(kgrp(c))

                def vgrp(sb, hf):
                    def f():
                        vs = (2 * B + sb) % 8
                        bank = next_bank()
                        P.wait("pe", bank_free[bank])
                        for dk in range(8):
                            pe_ev = P.op("pe", "matmul", ps[:, bank, :],
                                         lhsT=x1T[s][:, dk, sb * 128:(sb + 1) * 128],
                                         rhs=Win[:, dk, 2048 + hf * 512:2048 + (hf + 1) * 512],
                                         start=(dk == 0), stop=(dk == 7), sig=(dk == 7))
                        P.wait("act", pe_ev, v_rd[vs])
                        ev = P.op("act", "activation", Vr[:, vs, hf * 512:(hf + 1) * 512], ps[:, bank, :], AF.Copy)
                        bank_free[bank] = ev
                        kv_ev[B] = ev
                    return f
                for sb in range(2):
                    for hf in range(2):
                        items.append(vgrp(sb, hf))
                return items

            def items_qg(B):
                s = B % 2
                items = []

                def qgrp(c):
                    def f():
                        P.wait("pe", x1T_ev[s])
                        bank = next_bank()
                        P.wait("pe", bank_free[bank])
                        for dk in range(8):
                            pe_ev = P.op("pe", "matmul", ps[:, bank, 0:256], lhsT=Win[:, dk, c * 128:(c + 1) * 128],
                                         rhs=x1T[s][:, dk, :], start=(dk == 0), stop=(dk == 7), sig=(dk == 7))
                        P.wait("act", pe_ev, qt_rd[s])
                        ev = P.op("act", "activation", QT[s][:, c, :], ps[:, bank, 0:256], AF.Copy, scale=NA_SCALE)
                        bank_free[bank] = ev
                        q_ev[B] = ev
                    return f

                def ggrp(c):
                    def f():
                        bank = next_bank()
                        P.wait("pe", bank_free[bank])
                        for dk in range(8):
                            pe_ev = P.op("pe", "matmul", ps[:, bank, 0:256],
                                         lhsT=Win[:, dk, 3072 + c * 128:3072 + (c + 1) * 128],
                                         rhs=x1T[s][:, dk, :], start=(dk == 0), stop=(dk == 7), sig=(dk == 7))
                        if c == 7:
                            x1T_rd[s] = pe_ev
                        P.wait("act", pe_ev, th_rd[0])
                        tev = P.op("act", "activation", thb[c % 2], ps[:, bank, 0:256], AF.Tanh, scale=0.5)
                        P.wait("dve", tev, sg1_rd[s])
                        gev = P.op("dve", "scalar_tensor_tensor", out=sg1[s][:, c, :], in0=thb[c % 2], scalar=1.0,
                                   in1=ps[:, bank, 0:256], op0=ALU.add, op1=ALU.mult)
                        th_rd[0] = gev
                        bank_free[bank] = gev
                        g_ev[B] = gev
                    return f
                for c in range(8):
                    items.append(qgrp(c))
                for c in range(8):
                    items.append(ggrp(c))
                return items

            def items_out(B):
                s = B % 2
                items = []

                def og(tb):
                    def f():
                        q = blkq[0]
                        blkq[0] += 1
                        slot = q % 2
                        P.wait("sp", st_free[slot])
                        xr = P.dma("sp", "rb%d" % slot, rb[slot], x1s[B * 256 + tb * 128:B * 256 + tb * 128 + 128, :])
                        P.wait("pe", a_evs[B])
                        pevs = []
                        banks = []
                        for dh in range(2):
                            bank = next_bank()
                            banks.append(bank)
                            P.wait("pe", bank_free[bank])
                            for c in range(8):
                                pe_ev = P.op("pe", "matmul", ps[:, bank, :], lhsT=aT[s][:, c, tb * 128:(tb + 1) * 128],
                                             rhs=Wout[:, c, dh * 512:(dh + 1) * 512], start=(c == 0), stop=(c == 7),
                                             sig=(c == 7))
                            pevs.append(pe_ev)
                        aT_rd[0] = pe_ev

                        def cb(dh, e, banks=banks):
                            bank_free[banks[dh]] = e
                        ln_tail(slot, 1, pevs, banks, xr, y_d[name][B * 256 + tb * 128:B * 256 + tb * 128 + 128, :],
                                0, 128, cb)
                    return f
                for tb in range(2):
                    items.append(og(tb))
                return items

            def block_type(B):
                R = 4 * B
                if B == 0:
                    return [(2 * k, 0, 4, 7 - 2 * k) for k in range(4)], 1, 10, e_un
                if B == nblk - 1:
                    return [(nrows - 8 + 2 * k, 0, 4, 11 - 2 * k) for k in range(4)], 5, 10, e_un
                rng = [(0, 2), (0, 4), (0, 4), (0, 4), (1, 4), (3, 4)]
                return [(R - 4 + 2 * k, rng[k][0], rng[k][1], rng[k][0] + 11 - 2 * k) for k in range(6)], 4, 9, e_int

            e_ld = {}

            def issue_e_load(B, hg):
                pairs, j0, nj, etab = block_type(B)
                ei = hgc[0] % 2
                hgc[0] += 1
                P.wait("sp", eb_rd[ei])
                e_ld[(B, hg)] = (ei, P.dma("sp", "eb%d" % ei,
                                          Eb[ei][:, :, 0:nj * 64].rearrange("p h (j c) -> p h j c", c=64),
                                          etab[:, hg * 4:(hg + 1) * 4, j0:j0 + nj, :]))

            pending = []
            stb_free = [None]
            ssc = [0]

            def emit_ss(B, hg, ui, nss, prs):
                _, j0, nj, _ = block_type(B)
                s = B % 2
                u = ssc[0]
                ssc[0] += 1
                if hg == 0 and ui == 0:
                    P.wait("pe", kv_ev[min(B + 1, nblk - 1)], q_ev[B])
                P.wait("pe", stb_free[0])
                s_ev = None
                for slot, (a_, qa, qb, jp) in enumerate(prs):
                    n = (qb - qa) * 64
                    kslot = (a_ // 4) % 4
                    kcol = kslot * 256 + (a_ % 4) * 64
                    for j in range(4):
                        s_ev = P.op("pe", "matmul", ps[:, j, slot * 256:slot * 256 + n],
                                    lhsT=KT[32 * j:32 * j + 32, hg, kcol:kcol + 128],
                                    rhs=QT[s][32 * j:32 * j + 32, hg, qa * 64:qb * 64], start=True, stop=True,
                                    tile_position=(32 * j, 0), sig=(j == 3 and slot == len(prs) - 1))
                for slot, (a_, qa, qb, jp) in enumerate(prs):
                    kt_rd[(a_ // 4) % 4] = s_ev
                qt_rd[s] = s_ev
                ei, eld = e_ld[(B, hg)]
                infos = []
                for slot, (a_, qa, qb, jp) in enumerate(prs):
                    n = (qb - qa) * 64
                    pti = (u % 2) * 2 + slot
                    P.wait("act", s_ev, es_rd[slot])
                    x_ev = P.op("act", "activation", expS[slot][:, :, 0:n], ps[:, 0:4, slot * 256:slot * 256 + n], AF.Exp)
                    stb_free[0] = x_ev
                    P.wait("dve", x_ev, eld, pt_rd[pti])
                    p_ev = P.op("dve", "tensor_tensor", PT[pti][:, :, 0:n], expS[slot][:, :, 0:n],
                                Eb[ei][:, :, (jp - j0) * 64:(jp - j0) * 64 + n], ALU.mult)
                    es_rd[slot] = p_ev
                    eb_rd[ei] = p_ev
                    infos.append((a_, qa, qb, pti, p_ev, ui == 0 and slot == 0))
                pending.append((B, hg, ui == nss - 1, infos))

            def emit_pv():
                B, hg, last, infos = pending.pop(0)
                s = B % 2
                od = hg % 2
                pv_ev = None
                for (a_, qa, qb, pti, p_ev, first) in infos:
                    n = (qb - qa) * 64
                    vs = (a_ // 2) % 8
                    P.wait("pe", p_ev)
                    if first:
                        P.wait("pe", od_rd[od])
                    for j in range(4):
                        P.op("pe", "matmul", ps[32 * j:32 * j + 32, 4 + od, qa * 64:qb * 64],
                             lhsT=Vr[:, vs, (4 * hg + j) * 32:(4 * hg + j + 1) * 32], rhs=PT[pti][:, j, 0:n],
                             start=first, stop=False, tile_position=(0, 32 * j), skip_group_check=True, sig=False)
                    for j in range(4):
                        pv_ev = P.op("pe", "matmul", ps[32 * j:32 * j + 32, 4 + od, 256 + qa * 64:256 + qb * 64],
                                     lhsT=ones_bf, rhs=PT[pti][:, j, 0:n],
                                     start=False, stop=False, tile_position=(0, 32 * j), skip_group_check=True,
                                     sig=(j == 3))
                    pt_rd[pti] = pv_ev
                    v_rd[vs] = pv_ev
                if last:
                    P.wait("dve", pv_ev)
                    if OPT_RECIP:
                        r_ev = P.op("dve", "reciprocal_approx_accurate", rden, ps[:, 4 + od, 256:512], tmul)
                    else:
                        r_ev = P.op("dve", "reciprocal", rden, ps[:, 4 + od, 256:512])
                    P.wait("dve", r_ev, g_ev[B])
                    t_ev = P.op("dve", "scalar_tensor_tensor", out=tmul, in0=rden, scalar=0.5, in1=sg1[s][:, hg, :],
                                op0=ALU.mult, op1=ALU.mult)
                    sg1_rd[s] = t_ev
                    P.wait("dve", t_ev, aT_rd[0])
                    a_ev = P.op("dve", "tensor_tensor", aT[s][:, hg, :], ps[:, 4 + od, 0:256], tmul, ALU.mult)
                    od_rd[od] = a_ev
                    a_evs[B] = a_ev

            for it in items_proj_kv(0):
                it()
            if nblk > 1:
                for it in items_proj_kv(1):
                    it()
            for it in items_qg(0):
                it()
            issue_e_load(0, 0)
            for B in range(nblk):
                bg = []
                if B >= 1:
                    bg += items_out(B - 1)
                if B + 1 < nblk:
                    bg += items_qg(B + 1)
                if B + 2 < nblk:
                    bg += items_proj_kv(B + 2)
                pairs, j0, nj, etab = block_type(B)
                nss = len(pairs) // 2
                sss = [(hg, ui) for hg in range(8) for ui in range(nss)]
                nst = len(sss)
                done_bg = 0
                n_out = 2 if B >= 1 else 0
                costs = [3.5] * n_out
                for k in range(n_out, len(bg)):
                    kk = (k - n_out) % 32
                    costs.append(1.75 if (B + 1 < nblk and k - n_out >= 16 and kk - 16 >= 12) or
                                 (B + 1 >= nblk and kk >= 12) else 0.9)
                bg_cum = []
                acc = 0.0
                for k, cst in enumerate(costs):
                    acc += cst if k >= n_out else 0.0
                    bg_cum.append(acc)
                bg_total = acc + 1e-9
                for si, (hg, ui) in enumerate(sss):
                    if ui == 0:
                        if hg + 1 < 8:
                            issue_e_load(B, hg + 1)
                        elif B + 1 < nblk:
                            issue_e_load(B + 1, 0)
                    emit_ss(B, hg, ui, nss, pairs[2 * ui:2 * ui + 2])
                    if B >= 1 and si in (1, 2) and done_bg < si:
                        bg[done_bg]()
                        done_bg += 1
                    if si >= 3:
                        goal = bg_total * (si - 2) / (nst - 3)
                        while done_bg < len(bg) and bg_cum[done_bg] <= goal:
                            bg[done_bg]()
                            done_bg += 1
                    if len(pending) > 1:
                        emit_pv()
                while done_bg < len(bg):
                    bg[done_bg]()
                    done_bg += 1
            while pending:
                emit_pv()
            for it in items_out(nblk - 1):
                it()

        P.wait("pool", st_free[0], st_free[1])
        P.wait("sp", st_free[0], st_free[1])

    with nc.Block() as block_:
        @block_.tensor
        def _(eng):
            P.begin("pe", eng)
            program()

        @block_.scalar
        def _(eng):
            P.begin("act", eng)
            program()

        @block_.vector
        def _(eng):
            P.begin("dve", eng)
            program()

        @block_.gpsimd
        def _(eng):
            P.begin("pool", eng)
            program()

        @block_.sync
        def _(eng):
            P.begin("sp", eng)
            program()
    return nc


def _const_tables(seq_lens):
    ident = np.eye(128, dtype=np.float32)
    return ident


def _rtab(left_true, right_true):
    r = np.zeros((2, 4, 8), np.float32)
    for g, w in enumerate(POOL_WINDOWS):
        h = w // 2
        for c in range(8):
            r[0, g, c] = 1.0 / (min(c + h, 1 << 20) - max(c - h, 0)) if left_true else 1.0 / w
            t = 8 - c
            r[1, g, c] = 1.0 / (min(h, t) + h) if right_true else 1.0 / w
    return r


def _pool_mats(left_true, right_true):
    h = 8
    i = np.arange(128)[:, None]
    o = np.arange(128)[None, :]
    m = np.zeros((5, 128, 128), np.float32)
    m[0] = np.where(i - 128 >= o - h, 1.0 / 16, 0.0)
    cur = np.where((i >= o - h) & (i <= o + h - 1), 1.0 / 16, 0.0)
    m[2] = np.where(i + 128 <= o + h - 1, 1.0 / 16, 0.0)
    eye = np.eye(128, dtype=np.float32)
    m[1] = cur - eye
    first = cur.copy()
    last = cur.copy()
    if left_true:
        for oc in range(8, 16):
            first[:, oc] = np.where((i[:, 0] >= oc - h) & (i[:, 0] <= oc + h - 1), 1.0 / oc, 0.0)
    if right_true:
        for oc in range(112, 120):
            last[:, oc] = np.where((i[:, 0] >= oc - h) & (i[:, 0] <= oc + h - 1), 1.0 / (128 - oc), 0.0)
    m[3] = first - eye
    m[4] = last - eye
    return np.ascontiguousarray(np.transpose(m, (1, 0, 2)))


def _bias_master(rpb):
    kc = np.arange(64)[:, None]
    qc = np.arange(64)[None, :]
    cs = np.clip(qc - 8, 0, 48)
    cmask = (kc >= cs) & (kc < cs + 16)
    dc = np.clip(kc - qc + 15, 0, 30)
    bm = np.full((128, 32, 16, 64), NEG, np.float32)
    for half in range(2):
        for j in range(16):
            dr = 14 - j + half
            if dr < 0 or dr > 14:
                continue
            vals = rpb[:, dr, :][:, dc]
            vals = np.where(cmask[None], vals, np.float32(NEG))
            bm[half * 64:(half + 1) * 64, :, j, :] = np.transpose(vals, (1, 0, 2))
    return bm


_PROG_CACHE = {}


def run(seq_data, weights, n_cores):
    seqs = tuple((name, items[0][0].shape[0] - 16) for name, items in seq_data.items())
    key = (seqs, DEBUG, OPT_POOL, OPT_ROT, OPT_SQRT, OPT_RECIP, OPT_PEPOOL)
    if key not in _PROG_CACHE:
        _PROG_CACHE[key] = build_program(list(seqs))
    nc = _PROG_CACHE[key]
    f32 = lambda a: np.ascontiguousarray(np.asarray(a, dtype=np.float32))
    lnp = np.stack([weights["ln_g"][0], weights["ln_b"][0], weights["ln_g"][1], weights["ln_b"][1]], 0)
    common = {
        "w_in_pool": f32(weights["w_in_pool"][0]),
        "w_grp": f32(weights["w_grp_pool"][0]),
        "w_out_pool": f32(weights["w_out_pool"][0]),
        "w_in_na": f32(weights["w_in_na"][0]),
        "w_out_na": f32(weights["w_out_na"][0]),
        "scale_t": f32(np.asarray(weights["scale_pool"][0]).reshape(16, 128).T),
        "lnp": f32(np.broadcast_to(lnp[None], (128, 4, D))),
        "ident": np.eye(128, dtype=np.float32),
        "bias_master": _bias_master(np.asarray(weights["rpb_na"][0], np.float32)),
    }
    in_maps = []
    for c in range(n_cores):
        m = dict(common)
        rt = []
        pm = []
        for name, items in seq_data.items():
            xp, lt, rtrue = items[c]
            m["x_" + name] = f32(xp)
            rt.append(_rtab(lt, rtrue))
            pm.append(_pool_mats(lt, rtrue))
        m["rtab"] = f32(np.broadcast_to(np.stack(rt, 0)[None], (128, len(rt), 2, 4, 8)))
        m["pmat"] = f32(np.concatenate(pm, 1))
        in_maps.append(m)
    res = run_bass_kernel_spmd(nc, in_maps, core_ids=list(range(n_cores)))
    return res.results


def _pad_region(x, t0, t1):
    n = x.shape[0]
    out = np.zeros((t1 - t0 + 16, x.shape[1]), np.float32)
    lo, hi = max(t0 - 8, 0), min(t1 + 8, n)
    out[lo - (t0 - 8):hi - (t0 - 8)] = x[lo:hi]
    return out


SAMPLE_SPLIT_ROWS = 36


def kernel(x_prompt, x_sample, w_in_pool, w_grp_pool, scale_pool, w_out_pool,
           w_in_na, rpb_na, w_out_na, ln_g, ln_b):
    x_prompt = np.asarray(x_prompt)
    x_sample = np.asarray(x_sample)
    weights = dict(w_in_pool=np.asarray(w_in_pool), w_grp_pool=np.asarray(w_grp_pool),
                   scale_pool=np.asarray(scale_pool), w_out_pool=np.asarray(w_out_pool),
                   w_in_na=np.asarray(w_in_na), rpb_na=np.asarray(rpb_na), w_out_na=np.asarray(w_out_na),
                   ln_g=np.asarray(ln_g), ln_b=np.asarray(ln_b))
    nb = x_prompt.shape[0]
    ns = x_sample.shape[0]
    S = x_sample.shape[1]
    reg = SAMPLE_SPLIT_ROWS * GRID_W
    p_items = [(_pad_region(x_prompt[c % nb], 0, x_prompt.shape[1]), True, True) for c in range(N_CORES)]
    s_items = []
    for c in range(N_CORES):
        sq, half = c % ns, (c // ns) % 2
        t0 = 0 if half == 0 else S - reg
        s_items.append((_pad_region(x_sample[sq], t0, t0 + reg), half == 0, half == 1))
    results = run({"p": p_items, "s": s_items}, weights, N_CORES)
    y_prompt = np.stack([results[c]["y_p"] for c in range(nb)], 0).astype(np.float32)
    y_sample = np.empty(x_sample.shape, np.float32)
    for c in range(N_CORES):
        sq, half = c % ns, (c // ns) % 2
        if half == 0:
            y_sample[sq, :S // 2] = results[c]["y_s"][:S // 2]
        else:
            y_sample[sq, S // 2:] = results[c]["y_s"][reg - S // 2:]
    return (y_prompt, y_sample)
```

```python
import numpy as np
import concourse.bass as bass
import concourse.mybir as mybir
from concourse.bass_utils import run_bass_kernel_spmd

F32 = mybir.dt.float32
BF16 = mybir.dt.bfloat16
AF = mybir.ActivationFunctionType
ALU = mybir.AluOpType

D = 1024
GRID_W = 64
POOL_WINDOWS = (2, 4, 8, 16)
NA_SCALE = 32 ** -0.5
LN_EPS = 1e-5
ALPHA = 4 ** 0.25
NEG = -30000.0
UOFF = 16
UW = 544
N_CORES = 8
DEBUG = False
OPT_POOL = False
OPT_ROT = True
OPT_SQRT = True
OPT_RECIP = False
OPT_PEPOOL = True
ST_ENG = "pool"


class Prog:
    def __init__(self, nc, sems):
        self.nc = nc
        self.sems = sems
        self.active = None
        self.cur = None

    def begin(self, active, eng):
        self.active = active
        self.cur = eng
        self.cnt = {k: 0 for k in self.sems}
        self.waited = {}

    def op(self, eng, meth, *a, sig=True, **kw):
        ins = None
        if eng == self.active:
            ins = getattr(self.cur, meth)(*a, **kw)
        if sig:
            self.cnt[eng] += 1
            if ins is not None:
                ins.then_inc(self.sems[eng], 1)
            return (eng, self.cnt[eng])
        return None

    def wait(self, eng, *evs):
        for ev in evs:
            if ev is None:
                continue
            key, n = ev
            if self.waited.get((eng, key), 0) >= n:
                continue
            self.waited[(eng, key)] = n
            if eng == self.active:
                self.cur.wait_ge(self.sems[key], n)

    def dma(self, eng, semkey, out, in_):
        if eng == self.active:
            self.cur.dma_start(out=out, in_=in_).then_inc(self.sems[semkey], 16)
        self.cnt[semkey] += 16
        return (semkey, self.cnt[semkey])


class Arena:
    def __init__(self, nc, words):
        self.t = nc.alloc_sbuf_tensor("arena", [128, words], F32)
        self.off = 0
        self.words = words

    def f32(self, n):
        assert self.off + n <= self.words, (self.off, n, self.words)
        v = self.t[:, self.off:self.off + n]
        self.off += n
        return v

    def bf16(self, n):
        assert n % 2 == 0
        return self.f32(n // 2).bitcast(BF16)


def l0_tiles(ntok):
    tiles = []
    s = 0
    while s + 512 - 16 < ntok:
        tiles.append((s, 512))
        s += 496
    rem = ntok - s
    nt = ((rem + 16 + 127) // 128) * 128
    tiles.append((ntok + 16 - nt, nt))
    return tiles


def build_program(seqs):
    nc = bass.Bass("TRN2", target_bir_lowering=False)
    dram = {}

    def din(name, shape, dt=F32):
        dram[name] = nc.dram_tensor(name, list(shape), dt, kind="ExternalInput").ap()
        return dram[name]

    def dout(name, shape):
        dram[name] = nc.dram_tensor(name, list(shape), F32, kind="ExternalOutput").ap()
        return dram[name]

    xin_d = {}
    x1_d = {}
    y_d = {}
    for name, ntok in seqs:
        xin_d[name] = din("x_" + name, [ntok + 16, D])
        x1_d[name] = (nc.dram_tensor("x1_" + name, [ntok, D], F32, kind="ExternalOutput").ap() if DEBUG
                      else nc.dram_tensor("x1_" + name, [ntok, D], F32).ap())
        y_d[name] = dout("y_" + name, [ntok, D])
    w_in_pool = din("w_in_pool", [D, 4096])
    w_grp = din("w_grp", [4, 512, 512])
    w_out_pool = din("w_out_pool", [2048, D])
    w_in_na = din("w_in_na", [D, 4096])
    w_out_na = din("w_out_na", [D, D])
    scale_t = din("scale_t", [128, 16])
    lnp = din("lnp", [128, 4, D])
    ident_d = din("ident", [128, 128])
    rtab_d = din("rtab", [128, len(seqs), 2, 4, 8])
    bm_d = din("bias_master", [128, 32, 16, 64])
    pmat_d = din("pmat", [128, len(seqs) * 5, 128])
    e_un = nc.dram_tensor("e_un", [128, 32, 16, 64], BF16).ap()
    e_int = nc.dram_tensor("e_int", [128, 32, 16, 64], BF16).ap()

    sem_names = ["pe", "act", "dve", "pool", "xin", "rb0", "rb1", "st0", "st1", "wst0", "wst1", "wst2", "wst3", "wst4", "wst5",
                 "cst", "eb0", "eb1", "misc", "x1b0", "x1b1", "el0", "el1", "es0", "es1"]
    sems = {k: nc.alloc_semaphore(k) for k in sem_names}
    P = Prog(nc, sems)
    ar = Arena(nc, 53200)
    ps = nc.psum_tensor("ps", [128, 8, 512], F32).__enter__()

    Win = ar.bf16(8 * 4096).rearrange("p (k f) -> p k f", f=4096)
    Wout = ar.bf16(16 * 1024).rearrange("p (k f) -> p k f", f=1024)
    ident = ar.f32(128)
    gb = ar.f32(2 * D).rearrange("p (a f) -> p a f", f=D)
    scale_sb = ar.f32(16)
    rtab = ar.f32(64 * len(seqs)).rearrange("p (s e w c) -> p s e w c", s=len(seqs), e=2, w=4)
    ones_bf = ar.bf16(32)
    rb = [ar.f32(D), ar.f32(D)]
    stats = [ar.f32(12).rearrange("p (a b) -> p a b", b=6) for _ in range(2)]
    mv = [ar.f32(2) for _ in range(2)]
    rstd = [ar.f32(1) for _ in range(2)]
    nmr = [ar.f32(1) for _ in range(2)]
    nwt = [ar.f32(4) for _ in range(2)]
    eps_col = ar.f32(1)
    base = ar.off
    xin = ar.f32(4 * D).rearrange("p (b f) -> p b f", f=D)
    xT = ar.bf16(8 * 512).rearrange("p (k t) -> p k t", t=512)
    Ub = [ar.f32(UW) for _ in range(3)]
    Al = [ar.f32(UW) for _ in range(3)]
    Bl = Al
    u3 = ar.bf16(4 * 512).rearrange("p (b f) -> p b f", f=512)
    Pm = ar.bf16(len(seqs) * 5 * 128).rearrange("p (k t) -> p k t", t=128)
    mixed = ar.bf16(16 * 512).rearrange("p (k t) -> p k t", t=512)
    sgr = ar.bf16(3 * 512).rearrange("p (k t) -> p k t", t=512)
    zT = ar.bf16(16 * 512).rearrange("p (k t) -> p k t", t=512)
    etmp = ar.f32(8)
    etmp2 = ar.f32(8)
    wg_off = ar.off
    Wg = ar.bf16(16 * 512).rearrange("p (k f) -> p k f", f=512)
    l0_end = ar.off
    ar.off = base
    KT = ar.bf16(8 * 1024).rearrange("p (k t) -> p k t", t=1024)
    Vr = ar.bf16(8 * 1024).rearrange("p (s f) -> p s f", f=1024)
    QT = [ar.bf16(8 * 256).rearrange("p (k t) -> p k t", t=256) for _ in range(2)]
    sg1 = [ar.bf16(8 * 256).rearrange("p (k t) -> p k t", t=256) for _ in range(2)]
    aT1 = ar.bf16(8 * 256).rearrange("p (k t) -> p k t", t=256)
    aT = [aT1, aT1]
    x1b = [ar.f32(D), ar.f32(D)]
    x1T = [ar.bf16(8 * 256).rearrange("p (k t) -> p k t", t=256) for _ in range(2)]
    Eb = [ar.bf16(4 * 640).rearrange("p (h c) -> p h c", c=640) for _ in range(2)]
    expS = [ar.bf16(4 * 256).rearrange("p (h c) -> p h c", c=256) for _ in range(2)]
    PT = [ar.bf16(4 * 256).rearrange("p (h c) -> p h c", c=256) for _ in range(4)]
    rden = ar.f32(256)
    tmul = ar.f32(256)
    thb1 = ar.f32(256)
    thb = [thb1, thb1]
    l1_end = ar.off
    assert l1_end <= ar.words, (l1_end, ar.words)
    ar.off = base
    NSTG = 6
    stg = [ar.f32(2048) for _ in range(NSTG)]
    stg_end = ar.off
    ar.off = base
    bst = [ar.f32(4096), ar.f32(4096)]
    est = [ar.bf16(4096).rearrange("p (h j c) -> p h j c", h=4, j=16) for _ in range(2)]
    assert max(ar.off, stg_end) <= wg_off, (ar.off, stg_end, wg_off)

    dumps = {}
    if DEBUG == "l0t0":
        for nm, ap_, dt_ in (("xT", xT, BF16), ("Ub0", Ub[0], F32), ("mixed", mixed, BF16), ("sgr", sgr, BF16),
                             ("zT", zT, BF16), ("rb0", rb[0], F32), ("rb1", rb[1], F32), ("Win0", Win[:, 0, :], BF16),
                             ("Wg", Wg, BF16), ("Wout", Wout, BF16), ("scale_sb", scale_sb, F32), ("xin", xin, F32),
                             ("mv0", mv[0], F32), ("rstd0", rstd[0], F32), ("gb", gb, F32), ("Al3", Al[2], F32),
                             ("u3", u3, BF16), ("Pm", Pm, BF16)):
            dumps[nm] = (nc.dram_tensor("dbg_" + nm, list(ap_.shape), dt_, kind="ExternalOutput").ap(), ap_)

    def do_dumps():
        last = {k: (k, P.cnt[k]) for k in ("pe", "act", "dve", "pool")}
        for k in last:
            if last[k][1] > 0:
                P.wait("sp", last[k])
        P.wait("sp", st_free_g[0], st_free_g[1])
        ev = None
        for nm, (d_, a_) in dumps.items():
            ev = P.dma("sp", "misc", d_, a_)
        P.wait("sp", ev)

    st_free_g = [None, None]

    def program():
        c_ev = []
        c_ev.append(P.dma("sp", "cst", ident, ident_d))
        c_ev.append(P.dma("sp", "cst", gb, lnp[:, 0:2, :]))
        c_ev.append(P.dma("sp", "cst", scale_sb, scale_t))
        c_ev.append(P.dma("sp", "cst", rtab, rtab_d))
        cst_ev = c_ev[-1]
        for e in ("pe", "act", "dve", "pool"):
            P.wait(e, cst_ev)
        ones_ev = P.op("dve", "memset", ones_bf, 1.0)
        ones_ev = P.op("dve", "memset", eps_col, LN_EPS)
        P.wait("act", ones_ev)

        wst_free = [None] * NSTG
        wcount = [0]
        cast_engs = ["act", "dve", "pool"]

        def load_cast(dst_ap, src_ap, n):
            i = wcount[0]
            wcount[0] += 1
            s = i % NSTG
            P.wait("sp", wst_free[s])
            ev = P.dma("sp", "wst%d" % s, stg[s][:, 0:n] if len(src_ap.shape) == 2 else
                       stg[s][:, 0:n].rearrange("p (a b) -> p a b", b=src_ap.shape[2]), src_ap)
            ce = cast_engs[i % 3]
            P.wait(ce, ev)
            src = stg[s][:, 0:n]
            if len(dst_ap.shape) == 3:
                src = src.rearrange("p (a b) -> p a b", b=dst_ap.shape[2])
            if ce == "act":
                cev = P.op("act", "activation", dst_ap, src, AF.Copy)
            else:
                cev = P.op(ce, "tensor_copy", dst_ap, src)
            wst_free[s] = cev
            return cev

        def load_weights(layer):
            evs = []
            w_in = w_in_pool if layer == 0 else w_in_na
            wv = w_in.rearrange("(k p) f -> p k f", p=128)
            for k in range(8):
                for hf in range(2):
                    evs.append(load_cast(Win[:, k, hf * 2048:(hf + 1) * 2048],
                                         wv[:, k, hf * 2048:(hf + 1) * 2048], 2048))
            if layer == 0:
                gv = w_grp.rearrange("g (kc p) d -> p (g kc) d", p=128)
                for q in range(4):
                    evs.append(load_cast(Wg[:, q * 4:(q + 1) * 4, :], gv[:, q * 4:(q + 1) * 4, :], 2048))
                ov = w_out_pool.rearrange("(k p) d -> p k d", p=128)
                for q in range(8):
                    evs.append(load_cast(Wout[:, q * 2:(q + 1) * 2, :], ov[:, q * 2:(q + 1) * 2, :], 2048))
            else:
                ov = w_out_na.rearrange("(k p) d -> p k d", p=128)
                for q in range(4):
                    evs.append(load_cast(Wout[:, q * 2:(q + 1) * 2, :], ov[:, q * 2:(q + 1) * 2, :], 2048))
            return evs

        bst_free = [None, None]
        est_free = [None, None]
        for q in range(8):
            bb = q % 2
            bview = bst[bb].rearrange("p (h j c) -> p h j c", h=4, j=16)
            P.wait("sp", bst_free[bb])
            ev = P.dma("sp", "el%d" % bb, bview, bm_d[:, q * 4:(q + 1) * 4, :, :])
            P.wait("act", ev, est_free[bb])
            xe = P.op("act", "activation", est[bb], bview, AF.Exp)
            bst_free[bb] = xe
            P.wait("pool", xe)
            d1 = P.dma("pool", "es%d" % bb, e_un[:, q * 4:(q + 1) * 4, :, :], est[bb])
            P.wait("dve", d1, xe)
            m1 = P.op("dve", "memset", est[bb][0:64, :, 0:4, :], 0.0)
            m1 = P.op("dve", "memset", est[bb][0:64, :, 12:16, :], 0.0)
            m1 = P.op("dve", "memset", est[bb][64:128, :, 0:5, :], 0.0)
            m1 = P.op("dve", "memset", est[bb][64:128, :, 13:16, :], 0.0)
            P.wait("pool", m1)
            d2 = P.dma("pool", "es%d" % bb, e_int[:, q * 4:(q + 1) * 4, :, :], est[bb])
            est_free[bb] = d2
        for e in ("sp", "act", "dve", "pool"):
            P.wait(e, est_free[0], est_free[1])

        w0 = load_weights(0)
        for e in ("pe", "act", "dve", "pool", "sp"):
            P.wait(e, *w0[-3:])
        P.wait("sp", wst_free[0])
        pm_ld = P.dma("sp", "wst0", stg[0][:, 0:len(seqs) * 640].rearrange("p (k t) -> p k t", t=128), pmat_d)
        P.wait("dve", pm_ld)
        pm_ev = P.op("dve", "tensor_copy", Pm, stg[0][:, 0:len(seqs) * 640].rearrange("p (k t) -> p k t", t=128))
        wst_free[0] = pm_ev
        P.wait("pe", pm_ev)
        P.wait("sp", pm_ev)
        for u in Ub + Al:
            zl_ev = P.op("dve", "memset", u, 0.0)
        P.wait("act", zl_ev)

        st_free = [None, None]
        blk_counter = [0]

        def ln_tail(slot, layer, ps_evs, ps_banks, x_ev, dst_ap, p0, p1, last_reader_cb=None):
            r = rb[slot]
            e = None
            for dh in range(2):
                P.wait("dve", ps_evs[dh], x_ev)
                e = P.op("dve", "scalar_tensor_tensor", out=r[:, dh * 512:(dh + 1) * 512],
                         in0=r[:, dh * 512:(dh + 1) * 512], scalar=ALPHA,
                         in1=ps[:, ps_banks[dh], :], op0=ALU.mult, op1=ALU.add)
                if last_reader_cb:
                    last_reader_cb(dh, e)
                P.wait("dve", e)
                e = P.op("dve", "bn_stats", stats[slot][:, dh, :], r[:, dh * 512:(dh + 1) * 512])
            P.wait("dve", e)
            e = P.op("dve", "bn_aggr", mv[slot], stats[slot].rearrange("p a b -> p (a b)"))
            use_sqrt = OPT_SQRT
            if use_sqrt:
                P.wait("act", e)
                e = P.op("act", "activation", nwt[slot][:, 0:1], mv[slot][:, 1:2], AF.Sqrt, bias=eps_col, scale=1.0)
                P.wait("dve", e)
                e = P.op("dve", "reciprocal", rstd[slot], nwt[slot][:, 0:1])
            P.wait("dve", e)
            if not use_sqrt:
                e = P.op("dve", "tensor_scalar", nwt[slot][:, 0:1], mv[slot][:, 1:2], LN_EPS, None, ALU.add)
                P.wait("dve", e)
                e = P.op("dve", "tensor_scalar", nwt[slot][:, 1:2], nwt[slot][:, 0:1], 0.5, 0.5, ALU.mult, ALU.add)
                P.wait("dve", e)
                e = P.op("dve", "reciprocal", rstd[slot], nwt[slot][:, 1:2])
            for _ in range(0 if use_sqrt else 5):
                P.wait("dve", e)
                e = P.op("dve", "scalar_tensor_tensor", out=nwt[slot][:, 1:2], in0=rstd[slot],
                         scalar=nwt[slot][:, 0:1], in1=rstd[slot], op0=ALU.mult, op1=ALU.mult)
                P.wait("dve", e)
                e = P.op("dve", "tensor_scalar", nwt[slot][:, 2:3], nwt[slot][:, 1:2], -0.5, 1.5, ALU.mult, ALU.add)
                P.wait("dve", e)
                e = P.op("dve", "tensor_tensor", rstd[slot], rstd[slot], nwt[slot][:, 2:3], ALU.mult)
            P.wait("dve", e)
            e = P.op("dve", "scalar_tensor_tensor", out=nmr[slot], in0=mv[slot][:, 0:1], scalar=-1.0,
                     in1=rstd[slot], op0=ALU.mult, op1=ALU.mult)
            P.wait("act", e)
            e = P.op("act", "activation", r, r, AF.Identity, bias=nmr[slot], scale=rstd[slot])
            P.wait("pool", e)
            e = P.op("pool", "tensor_tensor", r, r, gb[:, 0, :], ALU.mult)
            P.wait("pool", e)
            e = P.op("pool", "tensor_tensor", r, r, gb[:, 1, :], ALU.add)
            P.wait(ST_ENG, e)
            st_free[slot] = P.dma(ST_ENG, "st%d" % slot, dst_ap, r[p0:p1, :])
            st_free_g[slot] = st_free[slot]

        bank_free = [None] * 8
        xin_free = None
        mixed_rd = [None] * 16
        zT_rd = None
        sg_rd = [None] * 3
        ub_rd = [None] * 3
        u3_rd = [None]
        al_rd = {"dve": None, "pool": None}
        ucnt = {"dve": 0, "pool": 0}
        xT_ready = None

        tiles = []
        seq_idx = {}
        for sidx, (name, ntok) in enumerate(seqs):
            seq_idx[name] = sidx
            tl = l0_tiles(ntok)
            for ti, (s, nt) in enumerate(tl):
                tiles.append((name, ntok, s, nt, ti == 0, ti == len(tl) - 1))

        def issue_xin_load(t):
            name, ntok, s, nt, _, _ = tiles[t]
            nb = nt // 128
            P.wait("sp", xin_free)
            return P.dma("sp", "xin", xin[:, 0:nb, :],
                         xin_d[name][s:s + nt, :].rearrange("(b p) f -> p b f", p=128))

        def transposes(t, ld_ev):
            nonlocal xin_free
            name, ntok, s, nt, _, _ = tiles[t]
            nb = nt // 128
            evs = []
            P.wait("pe", ld_ev)
            for dk in range(8):
                bank = 6 + dk % 2
                P.wait("pe", bank_free[bank])
                pe_ev = None
                for b in range(nb):
                    pe_ev = P.op("pe", "transpose", ps[:, bank, b * 128:(b + 1) * 128],
                                 xin[:, b, dk * 128:(dk + 1) * 128], ident, sig=(b == nb - 1))
                P.wait("act", pe_ev)
                ev = P.op("act", "activation", xT[:, dk, 0:nt], ps[:, bank, 0:nt], AF.Copy)
                bank_free[bank] = ev
                evs.append(ev)
            xin_free = pe_ev
            return evs[-1]

        rb_load_ev = {}

        def issue_rb_load(q, src_ap, p0, p1):
            slot = q % 2
            P.wait("sp", st_free[slot])
            rb_load_ev[q] = P.dma("sp", "rb%d" % slot, rb[slot][p0:p1, :], src_ap)

        def l0_rows(t, tb):
            name, ntok, s, nt, _, _ = tiles[t]
            lo = max(tb * 128, 8)
            hi = min(tb * 128 + 128, nt - 8)
            return name, s, lo - tb * 128, hi - tb * 128, s + lo - 8, s + hi - 8

        blocks0 = []
        for t, (name, ntok, s, nt, _, _) in enumerate(tiles):
            for tb in range(nt // 128):
                blocks0.append((t, tb))

        def issue_rb_load_l0(q):
            if q >= len(blocks0):
                return
            t, tb = blocks0[q]
            name, s, p0, p1, t0, t1 = l0_rows(t, tb)
            issue_rb_load(q, xin_d[name][s + tb * 128:s + tb * 128 + 128, :], 0, 128)

        ld = issue_xin_load(0)
        xT_ready = transposes(0, ld)
        issue_rb_load_l0(0)
        issue_rb_load_l0(1)
        q0 = 0

        for t, (name, ntok, s, nt, is_first, is_last) in enumerate(tiles):
            nb = nt // 128
            if t + 1 < len(tiles):
                ld_next = issue_xin_load(t + 1)
            P.wait("pe", xT_ready)
            m1_last = None
            mixed_ev = [None] * 16
            order = [0, 8, 4, 12, 1, 9, 5, 13, 2, 10, 6, 14, 3, 11, 7, 15]
            pe_pool_items = []
            if OPT_PEPOOL:
                order = [0, 8, 4, 1, 9, 5, 2, 10, 6, 3, 11, 7]
                sidx = seq_idx[name]
                for tb in range(nb):
                    bank = tb % 2
                    P.wait("pe", bank_free[bank])
                    for dk in range(8):
                        pe_ev = P.op("pe", "matmul", ps[:, bank, :], lhsT=xT[:, dk, tb * 128:(tb + 1) * 128],
                                     rhs=Win[:, dk, 1536:2048], start=(dk == 0), stop=(dk == 7), sig=(dk == 7))
                    P.wait("act", pe_ev, u3_rd[0] if tb == 0 else None)
                    u3_ev = P.op("act", "activation", u3[:, tb, :], ps[:, bank, :], AF.Copy)
                    bank_free[bank] = u3_ev

                def pe_pool(c, u3_ev=u3_ev):
                    fc = 12 + c
                    bank = 2 + c % 2
                    P.wait("pe", bank_free[bank], u3_ev)
                    pe_ev = None
                    for b in range(nb):
                        srcs = [bb for bb in (b - 1, b, b + 1) if 0 <= bb < nb]
                        for k, bb in enumerate(srcs):
                            if bb == b:
                                mi = 3 if (is_first and b == 0) else (4 if (is_last and b == nb - 1) else 1)
                            else:
                                mi = 0 if bb < b else 2
                            pe_ev = P.op("pe", "matmul", ps[:, bank, b * 128:(b + 1) * 128],
                                         lhsT=u3[:, bb, c * 128:(c + 1) * 128], rhs=Pm[:, sidx * 5 + mi, :],
                                         start=(k == 0), stop=(k == len(srcs) - 1),
                                         sig=(b == nb - 1 and k == len(srcs) - 1))
                    u3_rd[0] = pe_ev
                    P.wait("act", pe_ev, mixed_rd[fc])
                    ev = P.op("act", "activation", mixed[:, fc, 0:nt], ps[:, bank, 0:nt], AF.Copy)
                    bank_free[bank] = ev
                    mixed_ev[fc] = ev
                pe_pool_items = [lambda c=c: pe_pool(c) for c in range(4)]
            for oi, fc in enumerate(order):
                if pe_pool_items and oi in (2, 4, 6, 8):
                    pe_pool_items.pop(0)()
                bank = oi % 2
                P.wait("pe", bank_free[bank])
                for dk in range(8):
                    pe_ev = P.op("pe", "matmul", ps[:, bank, 0:nt], lhsT=Win[:, dk, fc * 128:(fc + 1) * 128],
                                 rhs=xT[:, dk, 0:nt], start=(dk == 0), stop=(dk == 7), sig=(dk == 7))
                g = fc // 4
                w = POOL_WINDOWS[g]
                h = w // 2
                on_pool = OPT_POOL and g < 2
                eng = "pool" if on_pool else "dve"
                ui_ = ucnt[eng] % 3
                ucnt[eng] += 1
                u = Ub[ui_]
                lv = Bl if on_pool else Al
                et = etmp2 if on_pool else etmp
                P.wait("act", pe_ev, ub_rd[ui_])
                cp = P.op("act", "activation", u[:, UOFF:UOFF + nt], ps[:, bank, 0:nt], AF.Copy)
                bank_free[bank] = cp
                P.wait(eng, cp, al_rd[eng], mixed_rd[fc])
                cur = u
                e = None
                lvl = 0
                sh = 1
                while sh < h:
                    lo = UOFF - h
                    hi = UOFF + nt + max(h - 2 * sh, 0)
                    e = P.op(eng, "tensor_tensor", lv[lvl][:, lo:hi], cur[:, lo:hi], cur[:, lo + sh:hi + sh], ALU.add)
                    P.wait(eng, e)
                    cur = lv[lvl]
                    lvl += 1
                    sh *= 2
                S = lv[lvl] if lvl < len(lv) else lv[0]
                e = P.op(eng, "tensor_tensor", S[:, UOFF:UOFF + nt], cur[:, UOFF - h:UOFF - h + nt],
                         cur[:, UOFF:UOFF + nt], ALU.add)
                P.wait(eng, e)
                if on_pool:
                    e = P.op(eng, "tensor_scalar", S[:, UOFF:UOFF + nt], S[:, UOFF:UOFF + nt], 1.0 / w, None, ALU.mult)
                    P.wait(eng, e)
                    e = P.op(eng, "tensor_tensor", mixed[:, fc, 0:nt], S[:, UOFF:UOFF + nt], u[:, UOFF:UOFF + nt],
                             ALU.subtract)
                    sfac = float(w)
                else:
                    e = P.op(eng, "scalar_tensor_tensor", out=mixed[:, fc, 0:nt], in0=S[:, UOFF:UOFF + nt],
                             scalar=1.0 / w, in1=u[:, UOFF:UOFF + nt], op0=ALU.mult, op1=ALU.subtract)
                    sfac = 1.0
                for edge, on, c0 in ((0, is_first, 8), (1, is_last, nt - 16)):
                    if on:
                        P.wait(eng, e)
                        e = P.op(eng, "tensor_tensor", et, S[:, UOFF + c0:UOFF + c0 + 8], rtab[:, seq_idx[name], edge, g, :], ALU.mult)
                        P.wait(eng, e)
                        if sfac != 1.0:
                            e = P.op(eng, "tensor_scalar", et, et, sfac, None, ALU.mult)
                            P.wait(eng, e)
                        e = P.op(eng, "tensor_tensor", mixed[:, fc, c0:c0 + 8], et,
                                 u[:, UOFF + c0:UOFF + c0 + 8], ALU.subtract)
                ub_rd[ui_] = e
                al_rd[eng] = e
                mixed_ev[fc] = e
            z_ev = None
            for dc in range(16):
                fc = 16 + dc
                bank = fc % 2
                P.wait("pe", bank_free[bank])
                for dk in range(8):
                    pe_ev = P.op("pe", "matmul", ps[:, bank, 0:nt], lhsT=Win[:, dk, fc * 128:(fc + 1) * 128],
                                 rhs=xT[:, dk, 0:nt], start=(dk == 0), stop=(dk == 7), sig=(dk == 7))
                m1_last = pe_ev
                P.wait("act", pe_ev, sg_rd[dc % 3])
                sg_ev = P.op("act", "activation", sgr[:, dc % 3, 0:nt], ps[:, bank, 0:nt], AF.Silu)
                bank_free[bank] = sg_ev
                g = dc // 4
                bank2 = 2 + dc % 2
                P.wait("pe", bank_free[bank2], *mixed_ev[4 * g:4 * g + 4])
                for kc in range(4):
                    pe2 = P.op("pe", "matmul", ps[:, bank2, 0:nt],
                               lhsT=Wg[:, g * 4 + kc, (dc % 4) * 128:(dc % 4 + 1) * 128],
                               rhs=mixed[:, g * 4 + kc, 0:nt], start=(kc == 0), stop=(kc == 3), sig=(kc == 3))
                if dc % 4 == 3:
                    for kc in range(4):
                        mixed_rd[g * 4 + kc] = pe2
                P.wait("dve", pe2, sg_ev, zT_rd)
                z_ev = P.op("dve", "scalar_tensor_tensor", out=zT[:, dc, 0:nt], in0=ps[:, bank2, 0:nt],
                            scalar=scale_sb[:, dc:dc + 1], in1=sgr[:, dc % 3, 0:nt], op0=ALU.mult, op1=ALU.mult)
                bank_free[bank2] = z_ev
                sg_rd[dc % 3] = z_ev
            if t + 1 < len(tiles):
                P.wait("act", m1_last)
                xT_ready = transposes(t + 1, ld_next)
            P.wait("pe", z_ev)
            for tb in range(nb):
                q = q0 + tb
                slot = q % 2
                pevs = []
                bpair = [(4, 5), (0, 1), (2, 3)][tb % 3] if OPT_ROT else (4, 5)
                for dh in range(2):
                    bank = bpair[dh]
                    P.wait("pe", bank_free[bank])
                    for wc in range(16):
                        pe_ev = P.op("pe", "matmul", ps[:, bank, :], lhsT=zT[:, wc, tb * 128:(tb + 1) * 128],
                                     rhs=Wout[:, wc, dh * 512:(dh + 1) * 512], start=(wc == 0), stop=(wc == 15),
                                     sig=(wc == 15))
                    pevs.append(pe_ev)
                zT_rd = pe_ev
                _, _, p0, p1, t0, t1 = l0_rows(t, tb)

                def cb(dh, e, bpair=bpair):
                    bank_free[bpair[dh]] = e
                ln_tail(slot, 0, pevs, list(bpair), rb_load_ev[q], x1_d[name][t0:t1, :], p0, p1, cb)
                issue_rb_load_l0(q + 2)
            q0 += nb
            if DEBUG == "l0t0":
                do_dumps()
                return

        P.wait("sp", st_free[0], st_free[1])
        P.wait("sp", zT_rd)
        for e in ("act", "dve", "pool"):
            P.wait(e, zT_rd, st_free[0], st_free[1])
        gl = P.dma("sp", "cst", gb, lnp[:, 2:4, :])
        w1 = load_weights(1)
        for e in ("act", "dve", "pool"):
            P.wait(e, gl)
        for e in ("pe", "act", "dve", "pool", "sp"):
            P.wait(e, *w1[-3:])
        z1 = P.op("dve", "memset", KT, 0.0)
        z1 = P.op("dve", "memset", Vr, 0.0)
        for e in ("pe", "act", "pool"):
            P.wait(e, z1)

        gen_bank = [6, 7]
        gb_i = [0]

        def next_bank():
            b = gen_bank[gb_i[0] % 2]
            gb_i[0] += 1
            return b

        th_rd = [None, None]
        kt_rd = [None] * 4
        v_rd = [None] * 8
        x1b_rd = [None, None]
        x1T_rd = [None, None]
        x1T_ev = [None, None]
        qt_rd = [None, None]
        sg1_rd = [None, None]
        aT_rd = [None, None]
        eb_rd = [None, None]
        es_rd = [None, None]
        pt_rd = [None] * 4
        od_rd = [None, None]
        st_free_b = [None, None]
        kv_ev = {}
        q_ev = {}
        g_ev = {}
        a_evs = {}
        blkq = [0]
        x1b_i = [0]
        stepc = [0]
        hgc = [0]
        LAG = 3

        for name, ntok in seqs:
            nrows = ntok // GRID_W
            nblk = nrows // 4
            x1s = x1_d[name]
            kv_ev.clear(); q_ev.clear(); g_ev.clear(); a_evs.clear()

            def items_proj_kv(B):
                s = B % 2
                slot = B % 4
                items = []

                def tr(sb, half):
                    def f():
                        if half == 0:
                            i = x1b_i[0]
                            x1b_i[0] += 1
                            bs = i % 2
                            tr.bs = bs
                            P.wait("sp", x1b_rd[bs])
                            tr.ld = P.dma("sp", "x1b%d" % bs, x1b[bs], x1s[B * 256 + sb * 128:B * 256 + sb * 128 + 128, :])
                            if sb == 0:
                                P.wait("act", x1T_rd[s])
                        bs = tr.bs
                        P.wait("pe", tr.ld)
                        bank = next_bank()
                        P.wait("pe", bank_free[bank])
                        for d4 in range(4):
                            dk = half * 4 + d4
                            pe_ev = P.op("pe", "transpose", ps[:, bank, d4 * 128:(d4 + 1) * 128],
                                         x1b[bs][:, dk * 128:(dk + 1) * 128], ident, sig=(d4 == 3))
                        P.wait("act", pe_ev)
                        ev = P.op("act", "activation",
                                  x1T[s][:, half * 4:half * 4 + 4, sb * 128:(sb + 1) * 128],
                                  ps[:, bank, :].rearrange("p (a b) -> p a b", b=128), AF.Copy)
                        bank_free[bank] = ev
                        if half == 1:
                            x1b_rd[bs] = pe_ev
                        x1T_ev[s] = ev
                    return f
                for sb in range(2):
                    for half in range(2):
                        items.append(tr(sb, half))

                def kgrp(c):
                    def f():
                        P.wait("pe", x1T_ev[s])
                        bank = next_bank()
                        P.wait("pe", bank_free[bank])
                        for dk in range(8):
                            pe_ev = P.op("pe", "matmul", ps[:, bank, 0:256],
                                         lhsT=Win[:, dk, 1024 + c * 128:1024 + (c + 1) * 128],
                                         rhs=x1T[s][:, dk, :], start=(dk == 0), stop=(dk == 7), sig=(dk == 7))
                        P.wait("act", pe_ev, kt_rd[slot])
                        ev = P.op("act", "activation", KT[:, c, slot * 256:(slot + 1) * 256], ps[:, bank, 0:256], AF.Copy)
                        bank_free[bank] = ev
                        kv_ev[B] = ev
                    return f
                for c in range(8):
                    items.append(kgrp(c))

                def vgrp(sb, hf):
                    def f():
                        vs = (2 * B + sb) % 8
                        bank = next_bank()
                        P.wait("pe", bank_free[bank])
                        for dk in range(8):
                            pe_ev = P.op("pe", "matmul", ps[:, bank, :],
                                         lhsT=x1T[s][:, dk, sb * 128:(sb + 1) * 128],
                                         rhs=Win[:, dk, 2048 + hf * 512:2048 + (hf + 1) * 512],
                                         start=(dk == 0), stop=(dk == 7), sig=(dk == 7))
                        P.wait("act", pe_ev, v_rd[vs])
                        ev = P.op("act", "activation", Vr[:, vs, hf * 512:(hf + 1) * 512], ps[:, bank, :], AF.Copy)
                        bank_free[bank] = ev
                        kv_ev[B] = ev
                    return f
                for sb in range(2):
                    for hf in range(2):
                        items.append(vgrp(sb, hf))
                return items

            def items_qg(B):
                s = B % 2
                items = []

                def qgrp(c):
                    def f():
                        P.wait("pe", x1T_ev[s])
                        bank = next_bank()
                        P.wait("pe", bank_free[bank])
                        for dk in range(8):
                            pe_ev = P.op("pe", "matmul", ps[:, bank, 0:256], lhsT=Win[:, dk, c * 128:(c + 1) * 128],
                                         rhs=x1T[s][:, dk, :], start=(dk == 0), stop=(dk == 7), sig=(dk == 7))
                        P.wait("act", pe_ev, qt_rd[s])
                        ev = P.op("act", "activation", QT[s][:, c, :], ps[:, bank, 0:256], AF.Copy, scale=NA_SCALE)
                        bank_free[bank] = ev
                        q_ev[B] = ev
                    return f

                def ggrp(c):
                    def f():
                        bank = next_bank()
                        P.wait("pe", bank_free[bank])
                        for dk in range(8):
                            pe_ev = P.op("pe", "matmul", ps[:, bank, 0:256],
                                         lhsT=Win[:, dk, 3072 + c * 128:3072 + (c + 1) * 128],
                                         rhs=x1T[s][:, dk, :], start=(dk == 0), stop=(dk == 7), sig=(dk == 7))
                        if c == 7:
                            x1T_rd[s] = pe_ev
                        P.wait("act", pe_ev, th_rd[0])
                        tev = P.op("act", "activation", thb[c % 2], ps[:, bank, 0:256], AF.Tanh, scale=0.5)
                        P.wait("dve", tev, sg1_rd[s])
                        gev = P.op("dve", "scalar_tensor_tensor", out=sg1[s][:, c, :], in0=thb[c % 2], scalar=1.0,
                                   in1=ps[:, bank, 0:256], op0=ALU.add, op1=ALU.mult)
                        th_rd[0] = gev
                        bank_free[bank] = gev
                        g_ev[B] = gev
                    return f
                for c in range(8):
                    items.append(qgrp(c))
                for c in range(8):
                    items.append(ggrp(c))
                return items

            def items_out(B):
                s = B % 2
                items = []

                def og(tb):
                    def f():
                        q = blkq[0]
                        blkq[0] += 1
                        slot = q % 2
                        P.wait("sp", st_free[slot])
                        xr = P.dma("sp", "rb%d" % slot, rb[slot], x1s[B * 256 + tb * 128:B * 256 + tb * 128 + 128, :])
                        P.wait("pe", a_evs[B])
                        pevs = []
                        banks = []
                        for dh in range(2):
                            bank = next_bank()
                            banks.append(bank)
                            P.wait("pe", bank_free[bank])
                            for c in range(8):
                                pe_ev = P.op("pe", "matmul", ps[:, bank, :], lhsT=aT[s][:, c, tb * 128:(tb + 1) * 128],
                                             rhs=Wout[:, c, dh * 512:(dh + 1) * 512], start=(c == 0), stop=(c == 7),
                                             sig=(c == 7))
                            pevs.append(pe_ev)
                        aT_rd[0] = pe_ev

                        def cb(dh, e, banks=banks):
                            bank_free[banks[dh]] = e
                        ln_tail(slot, 1, pevs, banks, xr, y_d[name][B * 256 + tb * 128:B * 256 + tb * 128 + 128, :],
                                0, 128, cb)
                    return f
                for tb in range(2):
                    items.append(og(tb))
                return items

            def block_type(B):
                R = 4 * B
                if B == 0:
                    return [(2 * k, 0, 4, 7 - 2 * k) for k in range(4)], 1, 10, e_un
                if B == nblk - 1:
                    return [(nrows - 8 + 2 * k, 0, 4, 11 - 2 * k) for k in range(4)], 5, 10, e_un
                rng = [(0, 2), (0, 4), (0, 4), (0, 4), (1, 4), (3, 4)]
                return [(R - 4 + 2 * k, rng[k][0], rng[k][1], rng[k][0] + 11 - 2 * k) for k in range(6)], 4, 9, e_int

            e_ld = {}

            def issue_e_load(B, hg):
                pairs, j0, nj, etab = block_type(B)
                ei = hgc[0] % 2
                hgc[0] += 1
                P.wait("sp", eb_rd[ei])
                e_ld[(B, hg)] = (ei, P.dma("sp", "eb%d" % ei,
                                          Eb[ei][:, :, 0:nj * 64].rearrange("p h (j c) -> p h j c", c=64),
                                          etab[:, hg * 4:(hg + 1) * 4, j0:j0 + nj, :]))

            pending = []
            stb_free = [None]
            ssc = [0]

            def emit_ss(B, hg, ui, nss, prs):
                _, j0, nj, _ = block_type(B)
                s = B % 2
                u = ssc[0]
                ssc[0] += 1
                if hg == 0 and ui == 0:
                    P.wait("pe", kv_ev[min(B + 1, nblk - 1)], q_ev[B])
                P.wait("pe", stb_free[0])
                s_ev = None
                for slot, (a_, qa, qb, jp) in enumerate(prs):
                    n = (qb - qa) * 64
                    kslot = (a_ // 4) % 4
                    kcol = kslot * 256 + (a_ % 4) * 64
                    for j in range(4):
                        s_ev = P.op("pe", "matmul", ps[:, j, slot * 256:slot * 256 + n],
                                    lhsT=KT[32 * j:32 * j + 32, hg, kcol:kcol + 128],
                                    rhs=QT[s][32 * j:32 * j + 32, hg, qa * 64:qb * 64], start=True, stop=True,
                                    tile_position=(32 * j, 0), sig=(j == 3 and slot == len(prs) - 1))
                for slot, (a_, qa, qb, jp) in enumerate(prs):
                    kt_rd[(a_ // 4) % 4] = s_ev
                qt_rd[s] = s_ev
                ei, eld = e_ld[(B, hg)]
                infos = []
                for slot, (a_, qa, qb, jp) in enumerate(prs):
                    n = (qb - qa) * 64
                    pti = (u % 2) * 2 + slot
                    P.wait("act", s_ev, es_rd[slot])
                    x_ev = P.op("act", "activation", expS[slot][:, :, 0:n], ps[:, 0:4, slot * 256:slot * 256 + n], AF.Exp)
                    stb_free[0] = x_ev
                    P.wait("dve", x_ev, eld, pt_rd[pti])
                    p_ev = P.op("dve", "tensor_tensor", PT[pti][:, :, 0:n], expS[slot][:, :, 0:n],
                                Eb[ei][:, :, (jp - j0) * 64:(jp - j0) * 64 + n], ALU.mult)
                    es_rd[slot] = p_ev
                    eb_rd[ei] = p_ev
                    infos.append((a_, qa, qb, pti, p_ev, ui == 0 and slot == 0))
                pending.append((B, hg, ui == nss - 1, infos))

            def emit_pv():
                B, hg, last, infos = pending.pop(0)
                s = B % 2
                od = hg % 2
                pv_ev = None
                for (a_, qa, qb, pti, p_ev, first) in infos:
                    n = (qb - qa) * 64
                    vs = (a_ // 2) % 8
                    P.wait("pe", p_ev)
                    if first:
                        P.wait("pe", od_rd[od])
                    for j in range(4):
                        P.op("pe", "matmul", ps[32 * j:32 * j + 32, 4 + od, qa * 64:qb * 64],
                             lhsT=Vr[:, vs, (4 * hg + j) * 32:(4 * hg + j + 1) * 32], rhs=PT[pti][:, j, 0:n],
                             start=first, stop=False, tile_position=(0, 32 * j), skip_group_check=True, sig=False)
                    for j in range(4):
                        pv_ev = P.op("pe", "matmul", ps[32 * j:32 * j + 32, 4 + od, 256 + qa * 64:256 + qb * 64],
                                     lhsT=ones_bf, rhs=PT[pti][:, j, 0:n],
                                     start=False, stop=False, tile_position=(0, 32 * j), skip_group_check=True,
                                     sig=(j == 3))
                    pt_rd[pti] = pv_ev
                    v_rd[vs] = pv_ev
                if last:
                    P.wait("dve", pv_ev)
                    if OPT_RECIP:
                        r_ev = P.op("dve", "reciprocal_approx_accurate", rden, ps[:, 4 + od, 256:512], tmul)
                    else:
                        r_ev = P.op("dve", "reciprocal", rden, ps[:, 4 + od, 256:512])
                    P.wait("dve", r_ev, g_ev[B])
                    t_ev = P.op("dve", "scalar_tensor_tensor", out=tmul, in0=rden, scalar=0.5, in1=sg1[s][:, hg, :],
                                op0=ALU.mult, op1=ALU.mult)
                    sg1_rd[s] = t_ev
                    P.wait("dve", t_ev, aT_rd[0])
                    a_ev = P.op("dve", "tensor_tensor", aT[s][:, hg, :], ps[:, 4 + od, 0:256], tmul, ALU.mult)
                    od_rd[od] = a_ev
                    a_evs[B] = a_ev

            for it in items_proj_kv(0):
                it()
            if nblk > 1:
                for it in items_proj_kv(1):
                    it()
            for it in items_qg(0):
                it()
            issue_e_load(0, 0)
            for B in range(nblk):
                bg = []
                if B >= 1:
                    bg += items_out(B - 1)
                if B + 1 < nblk:
                    bg += items_qg(B + 1)
                if B + 2 < nblk:
                    bg += items_proj_kv(B + 2)
                pairs, j0, nj, etab = block_type(B)
                nss = len(pairs) // 2
                sss = [(hg, ui) for hg in range(8) for ui in range(nss)]
                nst = len(sss)
                done_bg = 0
                for si, (hg, ui) in enumerate(sss):
                    if ui == 0:
                        if hg + 1 < 8:
                            issue_e_load(B, hg + 1)
                        elif B + 1 < nblk:
                            issue_e_load(B + 1, 0)
                    emit_ss(B, hg, ui, nss, pairs[2 * ui:2 * ui + 2])
                    if si == 1 and B >= 1:
                        while done_bg < 2:
                            bg[done_bg]()
                            done_bg += 1
                    if si >= 2:
                        target = (len(bg) * (si - 1) + (nst - 2) - 1) // (nst - 2)
                        while done_bg < min(target, len(bg)):
                            bg[done_bg]()
                            done_bg += 1
                    if len(pending) > 1:
                        emit_pv()
                while done_bg < len(bg):
                    bg[done_bg]()
                    done_bg += 1
            while pending:
                emit_pv()
            for it in items_out(nblk - 1):
                it()

        P.wait("pool", st_free[0], st_free[1])
        P.wait("sp", st_free[0], st_free[1])

    with nc.Block() as block_:
        @block_.tensor
        def _(eng):
            P.begin("pe", eng)
            program()

        @block_.scalar
        def _(eng):
            P.begin("act", eng)
            program()

        @block_.vector
        def _(eng):
            P.begin("dve", eng)
            program()

        @block_.gpsimd
        def _(eng):
            P.begin("pool", eng)
            program()

        @block_.sync
        def _(eng):
            P.begin("sp", eng)
            program()
    return nc


def _const_tables(seq_lens):
    ident = np.eye(128, dtype=np.float32)
    return ident


def _rtab(left_true, right_true):
    r = np.zeros((2, 4, 8), np.float32)
    for g, w in enumerate(POOL_WINDOWS):
        h = w // 2
        for c in range(8):
            r[0, g, c] = 1.0 / (min(c + h, 1 << 20) - max(c - h, 0)) if left_true else 1.0 / w
            t = 8 - c
            r[1, g, c] = 1.0 / (min(h, t) + h) if right_true else 1.0 / w
    return r


def _pool_mats(left_true, right_true):
    h = 8
    i = np.arange(128)[:, None]
    o = np.arange(128)[None, :]
    m = np.zeros((5, 128, 128), np.float32)
    m[0] = np.where(i - 128 >= o - h, 1.0 / 16, 0.0)
    cur = np.where((i >= o - h) & (i <= o + h - 1), 1.0 / 16, 0.0)
    m[2] = np.where(i + 128 <= o + h - 1, 1.0 / 16, 0.0)
    eye = np.eye(128, dtype=np.float32)
    m[1] = cur - eye
    first = cur.copy()
    last = cur.copy()
    if left_true:
        for oc in range(8, 16):
            first[:, oc] = np.where((i[:, 0] >= oc - h) & (i[:, 0] <= oc + h - 1), 1.0 / oc, 0.0)
    if right_true:
        for oc in range(112, 120):
            last[:, oc] = np.where((i[:, 0] >= oc - h) & (i[:, 0] <= oc + h - 1), 1.0 / (128 - oc), 0.0)
    m[3] = first - eye
    m[4] = last - eye
    return np.ascontiguousarray(np.transpose(m, (1, 0, 2)))


def _bias_master(rpb):
    kc = np.arange(64)[:, None]
    qc = np.arange(64)[None, :]
    cs = np.clip(qc - 8, 0, 48)
    cmask = (kc >= cs) & (kc < cs + 16)
    dc = np.clip(kc - qc + 15, 0, 30)
    bm = np.full((128, 32, 16, 64), NEG, np.float32)
    for half in range(2):
        for j in range(16):
            dr = 14 - j + half
            if dr < 0 or dr > 14:
                continue
            vals = rpb[:, dr, :][:, dc]
            vals = np.where(cmask[None], vals, np.float32(NEG))
            bm[half * 64:(half + 1) * 64, :, j, :] = np.transpose(vals, (1, 0, 2))
    return bm


_PROG_CACHE = {}


def run(seq_data, weights, n_cores):
    seqs = tuple((name, items[0][0].shape[0] - 16) for name, items in seq_data.items())
    key = (seqs, DEBUG, OPT_POOL, OPT_ROT, OPT_SQRT, OPT_RECIP, OPT_PEPOOL)
    if key not in _PROG_CACHE:
        _PROG_CACHE[key] = build_program(list(seqs))
    nc = _PROG_CACHE[key]
    f32 = lambda a: np.ascontiguousarray(np.asarray(a, dtype=np.float32))
    lnp = np.stack([weights["ln_g"][0], weights["ln_b"][0], weights["ln_g"][1], weights["ln_b"][1]], 0)
    common = {
        "w_in_pool": f32(weights["w_in_pool"][0]),
        "w_grp": f32(weights["w_grp_pool"][0]),
        "w_out_pool": f32(weights["w_out_pool"][0]),
        "w_in_na": f32(weights["w_in_na"][0]),
        "w_out_na": f32(weights["w_out_na"][0]),
        "scale_t": f32(np.asarray(weights["scale_pool"][0]).reshape(16, 128).T),
        "lnp": f32(np.broadcast_to(lnp[None], (128, 4, D))),
        "ident": np.eye(128, dtype=np.float32),
        "bias_master": _bias_master(np.asarray(weights["rpb_na"][0], np.float32)),
    }
    in_maps = []
    for c in range(n_cores):
        m = dict(common)
        rt = []
        pm = []
        for name, items in seq_data.items():
            xp, lt, rtrue = items[c]
            m["x_" + name] = f32(xp)
            rt.append(_rtab(lt, rtrue))
            pm.append(_pool_mats(lt, rtrue))
        m["rtab"] = f32(np.broadcast_to(np.stack(rt, 0)[None], (128, len(rt), 2, 4, 8)))
        m["pmat"] = f32(np.concatenate(pm, 1))
        in_maps.append(m)
    res = run_bass_kernel_spmd(nc, in_maps, core_ids=list(range(n_cores)))
    return res.results


def _pad_region(x, t0, t1):
    n = x.shape[0]
    out = np.zeros((t1 - t0 + 16, x.shape[1]), np.float32)
    lo, hi = max(t0 - 8, 0), min(t1 + 8, n)
    out[lo - (t0 - 8):hi - (t0 - 8)] = x[lo:hi]
    return out


SAMPLE_SPLIT_ROWS = 36


def kernel(x_prompt, x_sample, w_in_pool, w_grp_pool, scale_pool, w_out_pool,
           w_in_na, rpb_na, w_out_na, ln_g, ln_b):
    x_prompt = np.asarray(x_prompt)
    x_sample = np.asarray(x_sample)
    weights = dict(w_in_pool=np.asarray(w_in_pool), w_grp_pool=np.asarray(w_grp_pool),
                   scale_pool=np.asarray(scale_pool), w_out_pool=np.asarray(w_out_pool),
                   w_in_na=np.asarray(w_in_na), rpb_na=np.asarray(rpb_na), w_out_na=np.asarray(w_out_na),
                   ln_g=np.asarray(ln_g), ln_b=np.asarray(ln_b))
    nb = x_prompt.shape[0]
    ns = x_sample.shape[0]
    S = x_sample.shape[1]
    reg = SAMPLE_SPLIT_ROWS * GRID_W
    p_items = [(_pad_region(x_prompt[c % nb], 0, x_prompt.shape[1]), True, True) for c in range(N_CORES)]
    s_items = []
    for c in range(N_CORES):
        sq, half = c % ns, (c // ns) % 2
        t0 = 0 if half == 0 else S - reg
        s_items.append((_pad_region(x_sample[sq], t0, t0 + reg), half == 0, half == 1))
    results = run({"p": p_items, "s": s_items}, weights, N_CORES)
    y_prompt = np.stack([results[c]["y_p"] for c in range(nb)], 0).astype(np.float32)
    y_sample = np.empty(x_sample.shape, np.float32)
    for c in range(N_CORES):
        sq, half = c % ns, (c // ns) % 2
        if half == 0:
            y_sample[sq, :S // 2] = results[c]["y_s"][:S // 2]
        else:
            y_sample[sq, S // 2:] = results[c]["y_s"][reg - S // 2:]
    return (y_prompt, y_sample)
```
